# Optimizing a Trainium2 kernel written in Bass

```python
import jax, jax.numpy as jnp
from jax import lax
import numpy as np

D_MODEL = 2048
BATCH = 4
SEQ = 4096
DEPTH = 4

N_META = 16
BLOCK = 128
WINDOW = 128
HEAD_DIM = 64
SWA_HEADS = 16
SWA_KV_HEADS = 4
SWA_GROUP = SWA_HEADS // SWA_KV_HEADS
SB_HEADS = 16
SWA_WIDTH = SWA_HEADS * HEAD_DIM
SWA_KV_WIDTH = SWA_KV_HEADS * HEAD_DIM
SB_WIDTH = SB_HEADS * HEAD_DIM
MIX_WIDTH = SWA_WIDTH + SB_WIDTH
PROJ_WIDTH = SWA_WIDTH + 2 * SWA_KV_WIDTH + 3 * SB_WIDTH
D_FF = ((8 * D_MODEL + 3 * 256 - 1) // (3 * 256)) * 256
EPS = 1e-6
NEG = -1e30

kernel_name = "hybrid_swa_sink_stickbreaking_swiglu"


def rmsnorm(x, g):
    xf = x.astype(jnp.float32)
    y = xf * lax.rsqrt(jnp.mean(xf * xf, axis=-1, keepdims=True) + EPS)
    return (y * g.astype(jnp.float32)).astype(x.dtype)


def alibi_slopes(n_heads):
    h = jnp.arange(1, n_heads + 1, dtype=jnp.float32)
    return jnp.exp2(-8.0 * h / n_heads)


def swa_attention(q, k, v, sinks):
    b, L, _, d = q.shape
    nb = (L - N_META) // BLOCK
    f32 = jnp.float32
    scale = d ** -0.5
    slopes = alibi_slopes(SWA_HEADS).reshape(SWA_KV_HEADS, SWA_GROUP)[:, :, None, None]
    sink = sinks.astype(f32).reshape(SWA_KV_HEADS, SWA_GROUP)

    qg = q.reshape(b, L, SWA_KV_HEADS, SWA_GROUP, d)
    qm, km, vm = qg[:, :N_META], k[:, :N_META], v[:, :N_META]
    qr = qg[:, N_META:].reshape(b, nb, BLOCK, SWA_KV_HEADS, SWA_GROUP, d)
    kr = k[:, N_META:].reshape(b, nb, BLOCK, SWA_KV_HEADS, d)
    vr = v[:, N_META:].reshape(b, nb, BLOCK, SWA_KV_HEADS, d)
    pad = jnp.zeros_like(kr[:, :1])
    kb = jnp.concatenate([jnp.concatenate([pad, kr[:, :-1]], axis=1), kr], axis=2)
    vb = jnp.concatenate([jnp.concatenate([pad, vr[:, :-1]], axis=1), vr], axis=2)

    blk = jnp.arange(nb)[:, None, None]
    qi = jnp.arange(BLOCK)[None, :, None]
    kj = jnp.arange(2 * BLOCK)[None, None, :]
    delta = qi - kj + BLOCK
    key_real = blk * BLOCK + kj - BLOCK
    band_ok = (delta >= 0) & (delta < WINDOW) & (key_real >= 0)

    s_band = jnp.einsum('bnqhgd,bnkhd->bnhgqk', qr, kb).astype(f32) * scale
    s_band = s_band - slopes * delta.astype(f32)
    s_band = jnp.where(band_ok[None, :, None, None], s_band, NEG)

    t_pos = N_META + blk * BLOCK + qi
    delta_m = (t_pos - jnp.arange(N_META)[None, None, :]).astype(f32)
    s_meta = jnp.einsum('bnqhgd,bmhd->bnhgqm', qr, km).astype(f32) * scale
    s_meta = s_meta - slopes * delta_m[:, None, None]

    sink_col = jnp.broadcast_to(sink[None, None, :, :, None, None], s_band.shape[:-1] + (1,))
    p = jax.nn.softmax(jnp.concatenate([s_band, s_meta, sink_col], axis=-1), axis=-1)
    p_band = p[..., :2 * BLOCK].astype(v.dtype)
    p_meta = p[..., 2 * BLOCK:2 * BLOCK + N_META].astype(v.dtype)
    out_r = (jnp.einsum('bnhgqk,bnkhd->bnqhgd', p_band, vb)
             + jnp.einsum('bnhgqm,bmhd->bnqhgd', p_meta, vm))
    out_r = out_r.reshape(b, nb * BLOCK, SWA_WIDTH)

    mpos = jnp.arange(N_META)
    delta_mm = (mpos[:, None] - mpos[None, :])
    s_mm = jnp.einsum('bqhgd,bkhd->bhgqk', qm, km).astype(f32) * scale
    s_mm = s_mm - slopes * delta_mm.astype(f32)
    s_mm = jnp.where((delta_mm >= 0) & (delta_mm < WINDOW), s_mm, NEG)
    sink_mm = jnp.broadcast_to(sink[None, :, :, None, None], s_mm.shape[:-1] + (1,))
    p_mm = jax.nn.softmax(jnp.concatenate([s_mm, sink_mm], axis=-1), axis=-1)[..., :N_META]
    out_m = jnp.einsum('bhgqk,bkhd->bqhgd', p_mm.astype(v.dtype), vm).reshape(b, N_META, SWA_WIDTH)
    return jnp.concatenate([out_m, out_r], axis=1)


def sb_block(qb, qpos, k, v, kpos):
    d = qb.shape[-1]
    z = jnp.einsum('bqhd,bkhd->bhqk', qb, k).astype(jnp.float32) * (d ** -0.5)
    causal = kpos[None, :] < qpos[:, None]
    log1m = jnp.where(causal, -jax.nn.softplus(z), 0.0)
    after = lax.cumsum(log1m, axis=3, reverse=True) - log1m
    a = jnp.where(causal, jnp.exp(jax.nn.log_sigmoid(z) + after), 0.0)
    return jnp.einsum('bhqk,bkhd->bqhd', a.astype(v.dtype), v)


def stick_breaking_attention(q, k, v):
    b, L, h, d = q.shape
    nb = (L - N_META) // BLOCK
    kpos = jnp.arange(L)
    qr = q[:, N_META:].reshape(b, nb, BLOCK, h, d).swapaxes(0, 1)
    qpos = N_META + jnp.arange(nb * BLOCK).reshape(nb, BLOCK)
    out_r = lax.map(lambda xs: sb_block(xs[0], xs[1], k, v, kpos), (qr, qpos))
    out_r = out_r.swapaxes(0, 1).reshape(b, nb * BLOCK, h * d)
    mpos = jnp.arange(N_META)
    out_m = sb_block(q[:, :N_META], mpos, k[:, :N_META], v[:, :N_META], mpos).reshape(b, N_META, h * d)
    return jnp.concatenate([out_m, out_r], axis=1)


def setup_inputs(seed: int = 0) -> dict:
    key = jax.random.key(seed)
    ks = jax.random.split(key, 16)
    f32 = jnp.float32
    nrm = lambda k, shape, scale: jax.random.normal(k, shape, f32) * scale
    gain = lambda k, shape: 1.0 + 0.05 * jax.random.normal(k, shape, f32)
    return {
        "x": nrm(ks[0], (BATCH, SEQ, D_MODEL), 1.0),
        "meta_tokens": nrm(ks[1], (N_META, D_MODEL), 1.0),
        "attn_norm_g": gain(ks[2], (DEPTH, D_MODEL)),
        "w_in": nrm(ks[3], (DEPTH, D_MODEL, PROJ_WIDTH), D_MODEL ** -0.5),
        "q_norm_g": gain(ks[4], (DEPTH, HEAD_DIM)),
        "k_norm_g": gain(ks[5], (DEPTH, HEAD_DIM)),
        "attn_sinks": nrm(ks[6], (DEPTH, SWA_HEADS), 0.5),
        "swa_out_g": gain(ks[7], (DEPTH, SWA_WIDTH)),
        "sb_out_g": gain(ks[8], (DEPTH, SB_WIDTH)),
        "w_o": nrm(ks[9], (DEPTH, MIX_WIDTH, D_MODEL), MIX_WIDTH ** -0.5),
        "ffn_norm_g": gain(ks[10], (DEPTH, D_MODEL)),
        "w_gate": nrm(ks[11], (DEPTH, D_MODEL, D_FF), D_MODEL ** -0.5),
        "w_up": nrm(ks[12], (DEPTH, D_MODEL, D_FF), D_MODEL ** -0.5),
        "w_down": nrm(ks[13], (DEPTH, D_FF, D_MODEL), D_FF ** -0.5),
    }


def reference(x, meta_tokens, attn_norm_g, w_in, q_norm_g, k_norm_g, attn_sinks,
              swa_out_g, sb_out_g, w_o, ffn_norm_g, w_gate, w_up, w_down):
    b = x.shape[0]
    meta = jnp.broadcast_to(meta_tokens[None].astype(x.dtype), (b, N_META, D_MODEL))
    h = jnp.concatenate([meta, x], axis=1)
    L = h.shape[1]
    split_at = np.cumsum([SWA_WIDTH, SWA_KV_WIDTH, SWA_KV_WIDTH, SB_WIDTH, SB_WIDTH]).tolist()
    for l in range(DEPTH):
        hn = rmsnorm(h, attn_norm_g[l])
        proj = jnp.einsum('bld,dp->blp', hn, w_in[l])
        qa, ka, va, qb, kb, vb = jnp.split(proj, split_at, axis=-1)
        qa = rmsnorm(qa.reshape(b, L, SWA_HEADS, HEAD_DIM), q_norm_g[l])
        ka = rmsnorm(ka.reshape(b, L, SWA_KV_HEADS, HEAD_DIM), k_norm_g[l])
        va = va.reshape(b, L, SWA_KV_HEADS, HEAD_DIM)
        out_a = swa_attention(qa, ka, va, attn_sinks[l])
        out_b = stick_breaking_attention(qb.reshape(b, L, SB_HEADS, HEAD_DIM),
                                         kb.reshape(b, L, SB_HEADS, HEAD_DIM),
                                         vb.reshape(b, L, SB_HEADS, HEAD_DIM))
        mixed = jnp.concatenate([rmsnorm(out_a, swa_out_g[l]), rmsnorm(out_b, sb_out_g[l])], axis=-1)
        h = h + jnp.einsum('blm,md->bld', mixed, w_o[l])
        hn = rmsnorm(h, ffn_norm_g[l])
        g = jnp.einsum('bld,df->blf', hn, w_gate[l])
        u = jnp.einsum('bld,df->blf', hn, w_up[l])
        h = h + jnp.einsum('blf,fd->bld', jax.nn.silu(g) * u, w_down[l])
    return h[:, N_META:]
```

```python
import numpy as np
from contextlib import ExitStack
import concourse.bass as bass
import concourse.mybir as mybir
from concourse.bass_utils import run_bass_kernel_spmd

F32 = mybir.dt.float32
BF16 = mybir.dt.bfloat16
AF = mybir.ActivationFunctionType
ALU = mybir.AluOpType

D = 2048
KC = 16
PW = 4608
DFF = 5632
FC = 44
EPS = 1e-6
NEGB = -30000.0
NPV = 50
ENGS = ['pe', 'act', 'dve', 'pool', 'sp']
SAME_ENGINE_SYNC = True


class Plan:
    def __init__(self, nc, stack):
        self.nc = nc
        self.stack = stack
        self.ops = {e: [] for e in ENGS}
        self.last_w = {}
        self.readers = {}
        self.known = {e: {} for e in ENGS}
        self.epoch = 0
        self.sems = {}
        self.dma_cnt = {}

    def _sem(self, key):
        k = (key, self.epoch)
        if k not in self.sems:
            self.sems[k] = self.stack.enter_context(
                self.nc.semaphore("s%d_%s" % (self.epoch, key.replace(':', '_'))))
        return k

    def _deps(self, eng, reads, writes):
        deps = []
        for r in reads:
            d = self.last_w.get(r)
            if d is not None:
                deps.append(d)
        for w in writes:
            d = self.last_w.get(w)
            if d is not None:
                deps.append(d)
            deps.extend(self.readers.get(w, ()))
        waits = {}
        for (key, n) in deps:
            if key == eng and (eng == 'pe' or not SAME_ENGINE_SYNC):
                continue
            if self.known[eng].get(key, 0) >= n:
                continue
            if waits.get(key, 0) < n:
                waits[key] = n
        for key, n in waits.items():
            self.known[eng][key] = n
        return waits

    def _mark(self, me, reads, writes):
        for r in reads:
            self.readers.setdefault(r, []).append(me)
        for w in writes:
            self.last_w[w] = me
            self.readers[w] = []

    def op(self, eng, meth, *args, reads=(), writes=(), **kw):
        fn = (meth, args, kw)
        waits = self._deps(eng, reads, writes)
        semk = self._sem(eng)
        idx = len(self.ops[eng])
        rec = dict(kind='op', fn=fn, waits=[(self._sem(k), k, n) for k, n in waits.items()],
                   sem=semk, sig=False, idx=idx)
        self.ops[eng].append(rec)
        self._mark((eng, idx + 1), reads, writes)

    def dma(self, eng, semname, reads=(), writes=(), **kw):
        fn = ('dma_start', (), kw)
        key = 'dma:' + semname
        n_prev = self.dma_cnt.get((key, self.epoch), 0)
        waits = self._deps(eng, reads, writes)
        if n_prev > 0 and self.known[eng].get(key, 0) < n_prev:
            waits[key] = max(waits.get(key, 0), n_prev)
            self.known[eng][key] = n_prev
        semk = self._sem(key)
        n = n_prev + 1
        self.dma_cnt[(key, self.epoch)] = n
        rec = dict(kind='dma', fn=fn, waits=[(self._sem(k), k, m) for k, m in waits.items()],
                   sem=semk, idx=len(self.ops[eng]))
        self.ops[eng].append(rec)
        self._mark((key, n), reads, writes)

    def barrier(self, new_epoch=False):
        finals = []
        for e in ENGS:
            for rec in reversed(self.ops[e]):
                if rec['kind'] == 'op':
                    if rec['sem'][1] == self.epoch:
                        finals.append((e, rec['idx'] + 1))
                    break
        for (key, ep), n in self.dma_cnt.items():
            if ep == self.epoch:
                finals.append((key, n))
        for e in ENGS:
            waits = [(self._sem(key), key, n) for key, n in finals if key != e]
            self.ops[e].append(dict(kind='wait', waits=waits, idx=len(self.ops[e])))
        self.last_w = {}
        self.readers = {}
        if new_epoch:
            self.epoch += 1
            self.known = {e: {} for e in ENGS}
        else:
            for e in ENGS:
                for key, n in finals:
                    if key != e:
                        self.known[e][key] = max(self.known[e].get(key, 0), n)

    def emit(self):
        nc = self.nc
        for e in ENGS:
            for rec in self.ops[e]:
                for (semk, key, n) in rec['waits']:
                    if not key.startswith('dma:'):
                        self.ops[key][n - 1]['sig'] = True
        val = {}
        for e in ENGS:
            c = {}
            for rec in self.ops[e]:
                if rec['kind'] == 'op' and rec['sig']:
                    ep = rec['sem'][1]
                    c[ep] = c.get(ep, 0) + 1
                    val[(e, rec['idx'] + 1)] = c[ep]
        engobj = {'pe': 'tensor', 'act': 'scalar', 'dve': 'vector', 'pool': 'gpsimd', 'sp': 'sync'}
        with nc.Block() as block:
            for e in ENGS:
                def body(eng, e=e):
                    for rec in self.ops[e]:
                        for (semk, key, n) in rec['waits']:
                            v = 16 * n if key.startswith('dma:') else val[(key, n)]
                            eng.wait_ge(self.sems[semk], v)
                        if rec['kind'] == 'op':
                            m, a, k = rec['fn']
                            ins = getattr(eng, m)(*a, **k)
                            if rec['sig']:
                                ins.then_inc(self.sems[rec['sem']], 1)
                        elif rec['kind'] == 'dma':
                            m, a, k = rec['fn']
                            getattr(eng, m)(*a, **k).then_inc(self.sems[rec['sem']], 16)
                getattr(block, engobj[e])(body)


def groups_of(NB):
    gs = [(0, 1)]
    b = 1
    while b < NB:
        n = min(4, NB - b)
        gs.append((b, n))
        b += n
    return gs


def build(NB, DEPTH, debug=False, phases=None):
    LP = NB * 128
    nc = bass.Bass("TRN2", target_bir_lowering=False)
    dt_in = lambda name, shape: nc.dram_tensor(name, shape, F32, kind="ExternalInput").ap()
    xin = dt_in("xin", [LP, D])
    w_in = dt_in("w_in", [DEPTH, D, PW])
    w_o = dt_in("w_o", [DEPTH, D, D])
    w_gate = dt_in("w_gate", [DEPTH, D, DFF])
    w_up = dt_in("w_up", [DEPTH, D, DFF])
    w_down = dt_in("w_down", [DEPTH, DFF, D])
    pvec_d = dt_in("pvec", [128, DEPTH * NPV])
    sinks_d = dt_in("sinks", [1, DEPTH * 16])
    bias4_d = dt_in("bias4", [128, 4 * 16 * 128])
    ctab_d = dt_in("ctab", [128, NB * 16])
    sbm_d = dt_in("sbm", [128, 6 * 512])
    cbf_d = dt_in("cbf", [128, 3 * 128])
    cf32_d = dt_in("cf32", [128, 3 * 128])
    out_d = nc.dram_tensor("out", [LP - 128, D], F32, kind="ExternalOutput").ap()
    skind = "ExternalOutput" if debug else "Internal"
    scr = lambda name, shape, dt: nc.dram_tensor(name, shape, dt, kind=skind).ap()
    HT = scr("HT", [KC, 128, LP], F32)
    QTs = scr("QTs", [8, 128, LP], BF16)
    KTs = scr("KTs", [8, 128, LP], BF16)
    Vs = scr("Vs", [LP, 1024], BF16)
    QTa = scr("QTa", [8, 128, LP], BF16)
    KTa = scr("KTa", [4, 64, LP], BF16)
    Va = scr("Va", [LP, 256], BF16)
    MIXT = scr("MIXT", [KC, 128, LP], BF16)

    groups = groups_of(NB)

    with ExitStack() as st:
        sb = lambda name, shape, dt: st.enter_context(nc.sbuf_tensor("s_" + name, shape, dt))
        B_h = sb("B_h", [128, KC * 512], F32)
        B_x = sb("B_x", [128, KC * 512], BF16)
        NSL = 3
        SLW = max(8704, 2 * LP)
        slabs = [sb("slab%d" % i, [128, SLW], BF16) for i in range(NSL)]
        B_act = sb("B_act", [128, FC * 512], BF16)
        pvec = sb("pvec", [128, DEPTH * NPV], F32)
        esink = sb("esink", [128, DEPTH * 16], F32)
        ctab = sb("ctab", [128, NB * 16], F32)
        sbm = sb("sbm", [128, 6 * 512], BF16)
        cbf = sb("cbf", [128, 3 * 128], BF16)
        cf32 = sb("cf32", [128, 3 * 128], F32)
        wE = [sb("wE%d" % i, [128, 512], F32) for i in range(3)]
        wSP = [sb("wSP%d" % i, [128, 512], BF16) for i in range(3)]
        wG = [sb("wG%d" % i, [128, 512], F32) for i in range(2)]
        wA = [sb("wA%d" % i, [128, 512], BF16) for i in range(2)]
        wACC = [sb("wACC%d" % i, [128, 512], F32) for i in range(2)]
        wACCb = [sb("wACCb%d" % i, [128, 512], BF16) for i in range(2)]
        stg = [sb("stg%d" % i, [128, 512], BF16) for i in range(4)]
        wRS = wG[0]
        wT, nT_ = wG, 'wG%d'
        wR, nR_ = wE, 'wE%d'
        wSQ, nSQ_ = wACC, 'wACC%d'
        wRS2, nRS2_ = wG, 'wG%d'
        wSG, nSG_ = wE, 'wE%d'
        wPT = [wSP[0], wSP[1], wA[0]]
        nPT = ['wSP0', 'wSP1', 'wA0']
        ps = [st.enter_context(nc.psum_tensor("ps%d" % i, [128, 512], F32)) for i in range(8)]

        P = Plan(nc, st)
        TRI = cbf[:, 0:128]
        ONESNEG = cbf[:, 128:256]
        ONESB = cbf[:, 256:384]
        ONESF = cf32[:, 0:128]
        BDF = cf32[:, 128:256]
        IDF = cf32[:, 256:384]

        hT = B_h[:, :].rearrange("p (k t) -> p k t", k=KC)
        xT = B_x[:, :].rearrange("p (k t) -> p k t", k=KC)
        actT = B_act[:, :].rearrange("p (k t) -> p k t", k=FC)
        bias4 = B_act[:, 0:16384].bitcast(F32).rearrange("p (s h t) -> p s h t", s=4, h=16)
        KA = B_act[:, 16384:16384 + 3072].rearrange("p (j t) -> p j t", j=4)
        VA = B_act[:, 19456:19456 + 3072].rearrange("p (b j d) -> p b j d", b=6, j=4)
        Qs = B_x[:, 0:4096].rearrange("p (c t) -> p c t", c=8)
        Qa16 = slabs[2][:, 0:8192].rearrange("p (h t) -> p h t", h=16)
        tokbuf = B_act[:, 0:4096].bitcast(F32)

        stg_i = [0]

        def next_stg():
            i = stg_i[0] % 4
            stg_i[0] += 1
            return i

        slab_i = [0]

        def load_slab(src_ap, nk, ncols):
            i = slab_i[0] % NSL
            slab_i[0] += 1
            view = slabs[i][:, 0:nk * ncols].rearrange("p (k n) -> p k n", k=nk)
            src = src_ap.rearrange("(k p) n -> p k n", p=128)
            P.dma('pool', 'slab%d' % i, out=view, in_=src, writes=['slab%d' % i])
            return view, 'slab%d' % i

        P.dma('sp', 'c0', out=pvec[:, :], in_=pvec_d[:, :], writes=['pvec'])
        P.dma('sp', 'c1', out=ctab[:, :], in_=ctab_d[:, :], writes=['ctab'])
        P.dma('sp', 'c2', out=cf32[:, :], in_=cf32_d[:, :], writes=['cf32'])
        P.dma('sp', 'c3', out=esink[:, :], in_=sinks_d[:, :].broadcast_to([128, DEPTH * 16]), writes=['esink'])
        P.dma('pool', 'c4', out=sbm[:, :], in_=sbm_d[:, :], writes=['sbm'])
        P.dma('pool', 'c5', out=cbf[:, :], in_=cbf_d[:, :], writes=['cbf'])
        P.op('act', 'activation', out=esink[:, :], in_=esink[:, :], func=AF.Exp, reads=['esink'], writes=['esink'])

        qsc = sb("qsc", [128, DEPTH], F32)
        for l_ in range(DEPTH):
            P.op('act', 'activation', out=qsc[:, l_:l_ + 1], in_=pvec[:, l_ * NPV + 48:l_ * NPV + 49], func=AF.Copy,
                 scale=0.125, reads=['pvec'], writes=['qsc'])

        def gcol(l, j):
            return pvec[:, l * NPV + j: l * NPV + j + 1]

        def rstd_from_ps(psb, nt, inv_n, out_tile, wname, psname):
            P.op('act', 'activation', out=out_tile[:, 0:nt], in_=psb[:, 0:nt], func=AF.Sqrt, scale=inv_n, bias=EPS,
                 reads=[psname], writes=[wname])
            P.op('dve', 'reciprocal', out=out_tile[:, 0:nt], in_=out_tile[:, 0:nt], reads=[wname], writes=[wname])

        sq_i = [0]

        def norm_stats(src_view, nk, nt, psb, psname, src_name):
            for k in range(nk):
                i = sq_i[0] % 2
                sq_i[0] += 1
                P.op('act', 'activation', out=wSQ[i][:, 0:nt], in_=src_view[:, k, 0:nt], func=AF.Square,
                     reads=[src_name], writes=[nSQ_ % i])
                P.op('pe', 'matmul', psb[:, 0:nt], lhsT=ONESF, rhs=wSQ[i][:, 0:nt], start=(k == 0), stop=(k == nk - 1),
                     reads=[nSQ_ % i, 'cf32'], writes=[psname])

        def big_rmsnorm(l, gbase, nt):
            norm_stats(hT, KC, nt, ps[6], 'ps6', 'hT')
            rstd_from_ps(ps[6], nt, 1.0 / D, wRS, 'wG0', 'ps6')
            for kc in range(KC):
                P.op('dve', 'scalar_tensor_tensor', out=xT[:, kc, 0:nt], in0=hT[:, kc, 0:nt], scalar=gcol(l, gbase + kc),
                     in1=wRS[:, 0:nt], op0=ALU.mult, op1=ALU.mult, reads=['hT', 'wG0', 'pvec'], writes=['xT'])

        def store_chunk(dst_ap, src_psb, psname, nt):
            i = next_stg()
            P.op('act', 'activation', out=stg[i][:, 0:nt], in_=src_psb[:, 0:nt], func=AF.Copy,
                 reads=[psname], writes=['stg%d' % i])
            P.dma('sp', 'stg%d' % i, out=dst_ap, in_=stg[i][:, 0:nt], reads=['stg%d' % i])

        psr = [0]

        def next_ps4():
            i = psr[0] % 4
            psr[0] += 1
            return i

        def phase0():
            for b in range(NB):
                P.dma('sp', 'tok', out=tokbuf, in_=xin[b * 128:(b + 1) * 128, :], writes=['tokbuf'])
                for q in range(4):
                    pi = next_ps4()
                    for j in range(4):
                        kc = q * 4 + j
                        P.op('pe', 'transpose', ps[pi][:, j * 128:(j + 1) * 128], tokbuf[:, kc * 128:(kc + 1) * 128], IDF,
                             reads=['tokbuf', 'cf32'], writes=['ps%d' % pi])
                    src = ps[pi][:, :].rearrange("p (a t) -> p a t", a=4)
                    dst = hT[:, q * 4:(q + 1) * 4, 0:128]
                    if q % 2 == 0:
                        P.op('act', 'activation', out=dst, in_=src, func=AF.Copy, reads=['ps%d' % pi], writes=['hT'])
                    else:
                        P.op('dve', 'tensor_copy', out=dst, in_=src, reads=['ps%d' % pi], writes=['hT'])
                P.dma('sp', 'hst', out=HT[:, :, b * 128:(b + 1) * 128].rearrange("k p t -> p k t"), in_=hT[:, :, 0:128],
                      reads=['hT'])

        def phaseF():
            tb4 = tokbuf.rearrange("p (q n) -> p q n", q=4)
            for b in range(1, NB):
                P.dma('sp', 'hT', out=hT[:, :, 0:128], in_=HT[:, :, b * 128:(b + 1) * 128].rearrange("k p t -> p k t"),
                      writes=['hT'])
                for q in range(4):
                    pi = next_ps4()
                    for j in range(4):
                        kc = q * 4 + j
                        P.op('pe', 'transpose', ps[pi][:, j * 128:(j + 1) * 128], hT[:, kc, 0:128], IDF,
                             reads=['hT', 'cf32'], writes=['ps%d' % pi])
                    if q % 2 == 0:
                        P.op('act', 'activation', out=tb4[:, q, :], in_=ps[pi][:, :], func=AF.Copy,
                             reads=['ps%d' % pi], writes=['tokbuf'])
                    else:
                        P.op('dve', 'tensor_copy', out=tb4[:, q, :], in_=ps[pi][:, :], reads=['ps%d' % pi], writes=['tokbuf'])
                P.dma('sp', 'tok', out=out_d[(b - 1) * 128:b * 128, :], in_=tokbuf, reads=['tokbuf'])

        def phase1(l):
            for (b0, ncb) in groups:
                t0 = b0 * 128
                nt = ncb * 128
                P.dma('sp', 'hT', out=hT[:, :, 0:nt], in_=HT[:, :, t0:t0 + nt].rearrange("k p t -> p k t"), writes=['hT'])
                big_rmsnorm(l, 0, nt)
                for s in range(9):
                    wv, wname = load_slab(w_in[l, :, s * 512:(s + 1) * 512], KC, 512)
                    if s <= 6:
                        noc = 4 if s != 2 else 2
                        for oc in range(noc):
                            pi = next_ps4()
                            pn = 'ps%d' % pi
                            for kc in range(KC):
                                P.op('pe', 'matmul', ps[pi][:, 0:nt], lhsT=wv[:, kc, oc * 128:(oc + 1) * 128],
                                     rhs=xT[:, kc, 0:nt], start=(kc == 0), stop=(kc == KC - 1),
                                     reads=[wname, 'xT'], writes=[pn])
                            if s in (3, 4):
                                store_chunk(QTs[(s - 3) * 4 + oc, :, t0:t0 + nt], ps[pi], pn, nt)
                            elif s in (5, 6):
                                store_chunk(KTs[(s - 5) * 4 + oc, :, t0:t0 + nt], ps[pi], pn, nt)
                            else:
                                i = sq_i[0] % 2
                                sq_i[0] += 1
                                pj = 4 + i
                                pjn = 'ps%d' % pj
                                P.op('act', 'activation', out=wSQ[i][:, 0:nt], in_=ps[pi][:, 0:nt], func=AF.Square,
                                     reads=[pn], writes=[nSQ_ % i])
                                P.op('pe', 'matmul', ps[pj][:, 0:nt], lhsT=BDF, rhs=wSQ[i][:, 0:nt], start=True, stop=True,
                                     reads=[nSQ_ % i, 'cf32'], writes=[pjn])
                                rstd_from_ps(ps[pj], nt, 1.0 / 64, wRS2[i], nRS2_ % i, pjn)
                                si = next_stg()
                                gc = qsc[:, l:l + 1] if s < 2 else gcol(l, 49)
                                P.op('dve', 'scalar_tensor_tensor', out=stg[si][:, 0:nt], in0=ps[pi][:, 0:nt], scalar=gc,
                                     in1=wRS2[i][:, 0:nt], op0=ALU.mult, op1=ALU.mult,
                                     reads=[pn, nRS2_ % i, 'pvec', 'qsc'], writes=['stg%d' % si])
                                if s < 2:
                                    P.dma('sp', 'stg%d' % si, out=QTa[s * 4 + oc, :, t0:t0 + nt], in_=stg[si][:, 0:nt],
                                          reads=['stg%d' % si])
                                else:
                                    P.dma('sp', 'stg%d' % si,
                                          out=KTa[2 * oc:2 * oc + 2, :, t0:t0 + nt].rearrange("j p t -> (j p) t"),
                                          in_=stg[si][:, 0:nt], reads=['stg%d' % si])
                    if s in (2, 7, 8):
                        c0, ncol = (256, 256) if s == 2 else (0, 512)
                        for tb in range(ncb):
                            pi = next_ps4()
                            pn = 'ps%d' % pi
                            for kc in range(KC):
                                P.op('pe', 'matmul', ps[pi][:, 0:ncol], lhsT=xT[:, kc, tb * 128:(tb + 1) * 128],
                                     rhs=wv[:, kc, c0:c0 + ncol], start=(kc == 0), stop=(kc == KC - 1),
                                     reads=[wname, 'xT'], writes=[pn])
                            r0 = t0 + tb * 128
                            if s == 2:
                                store_chunk(Va[r0:r0 + 128, :], ps[pi], pn, 256)
                            else:
                                store_chunk(Vs[r0:r0 + 128, (s - 7) * 512:(s - 6) * 512], ps[pi], pn, 512)

        def phase2(l):
            P.dma('sp', 'bias', out=B_act[:, 0:16384].bitcast(F32), in_=bias4_d[:, :], writes=['bias4'])
            kv_i = [0]
            tile_i = [0]
            for (b0, ncb) in groups:
                t0 = b0 * 128
                nt = ncb * 128
                P.dma('sp', 'qs', out=Qs[:, :, 0:nt], in_=QTs[:, :, t0:t0 + nt].rearrange("c p t -> p c t"), writes=['Qs'])
                P.dma('sp', 'qa', out=Qa16[0:64, :, 0:nt],
                      in_=QTa.rearrange("c (two p) t -> (c two) p t", two=2)[:, :, t0:t0 + nt].rearrange("h p t -> p h t"),
                      writes=['slab2'])
                kb_lo = max(b0 - 1, 0)
                nkb = b0 + ncb - kb_lo
                P.dma('sp', 'ka0', out=KA[0:64, :, 0:128], in_=KTa[:, :, 0:128].rearrange("j p t -> p j t"),
                      writes=['KA'])
                P.dma('sp', 'ka0', out=KA[0:64, :, 128:128 + nkb * 128],
                      in_=KTa[:, :, kb_lo * 128:(kb_lo + nkb) * 128].rearrange("j p t -> p j t"), writes=['KA'])
                for half in range(2):
                    hs = slice(half * 64, (half + 1) * 64)
                    P.dma('sp', 'va%d' % half, out=VA[:, 0, :, hs], in_=Va[0:128, :].rearrange("p (j d) -> p j d", j=4),
                          writes=['VA'])
                    for kk in range(nkb):
                        P.dma('sp', 'va%d' % half, out=VA[:, 1 + kk, :, hs],
                              in_=Va[(kb_lo + kk) * 128:(kb_lo + kk + 1) * 128, :].rearrange("p (j d) -> p j d", j=4),
                              writes=['VA'])
                for bi in range(ncb if 'a' in P2PARTS else 0):
                    b = b0 + bi
                    cb = bi * 128
                    if b == 0:
                        segs = [(3, 0, 0)]
                    else:
                        segs = []
                        if b >= 2:
                            segs.append((1, 128 + (b - 1 - kb_lo) * 128, 1 + (b - 1 - kb_lo)))
                        segs.append((0, 128 + (b - kb_lo) * 128, 1 + (b - kb_lo)))
                        segs.append((2, 0, 0))
                    ns = len(segs)
                    for j in range(4):
                        for si, (bidx, kcol, vblk) in enumerate(segs):
                            pi = si
                            pn = 'ps%d' % pi
                            for u in range(4):
                                h = 4 * j + u
                                hb = (u % 2) * 64
                                P.op('pe', 'matmul', ps[pi][:, u * 128:(u + 1) * 128], lhsT=KA[0:64, j, kcol:kcol + 128],
                                     rhs=Qa16[0:64, h, cb:cb + 128], start=True, stop=True,
                                     reads=['KA', 'slab2'], writes=[pn])
                            ti = tile_i[0] % 2
                            tile_i[0] += 1
                            tv = wT[ti][:, :].rearrange("p (u t) -> p u t", u=4)
                            if '4' not in P2PARTS:
                                continue
                            P.op('dve', 'tensor_tensor', out=tv, in0=ps[pi][:, :].rearrange("p (u t) -> p u t", u=4),
                                 in1=bias4[:, bidx, 4 * j:4 * j + 4, :], op=ALU.add,
                                 reads=[pn, 'bias4'], writes=[nT_ % ti])
                            if bidx == 2 and '1' in P2PARTS:
                                P.op('dve', 'tensor_tensor', out=tv, in0=tv,
                                     in1=ctab[:, b * 16 + 4 * j:b * 16 + 4 * j + 4].unsqueeze(2).broadcast_to([128, 4, 128]),
                                     op=ALU.add, reads=[nT_ % ti, 'ctab'], writes=[nT_ % ti])
                            if '5' in P2PARTS:
                                P.op('act', 'activation', out=wPT[si][:, :], in_=wT[ti][:, :], func=AF.Exp,
                                     reads=[nT_ % ti], writes=[nPT[si]])
                        for u in range(4 if '2' in P2PARTS else 0):
                            for si, (bidx, kcol, vblk) in enumerate(segs):
                                P.op('pe', 'matmul', ps[3][:, u * 128:(u + 1) * 128], lhsT=VA[:, vblk, j, :],
                                     rhs=wPT[si][:, u * 128:(u + 1) * 128], start=(si == 0), stop=(si == ns - 1),
                                     reads=['VA', nPT[si]], writes=['ps3'])
                        for u in range(4 if '2' in P2PARTS else 0):
                            for si, (bidx, kcol, vblk) in enumerate(segs):
                                P.op('pe', 'matmul', ps[4][:, u * 128:(u + 1) * 128], lhsT=ONESB,
                                     rhs=wPT[si][:, u * 128:(u + 1) * 128], start=(si == 0), stop=(si == ns - 1),
                                     reads=['cbf', nPT[si]], writes=['ps4'])
                        if '3' not in P2PARTS:
                            continue
                        ri = tile_i[0] % 2
                        rv = wR[ri][:, :].rearrange("p (u t) -> p u t", u=4)
                        P.op('dve', 'tensor_tensor', out=rv, in0=ps[4][:, :].rearrange("p (u t) -> p u t", u=4),
                             in1=esink[:, l * 16 + 4 * j:l * 16 + 4 * j + 4].unsqueeze(2).broadcast_to([128, 4, 128]),
                             op=ALU.add, reads=['ps4', 'esink'], writes=[nR_ % ri])
                        P.op('dve', 'reciprocal', out=wR[ri][:, :], in_=wR[ri][:, :], reads=[nR_ % ri], writes=[nR_ % ri])
                        for par in range(2):
                            pb = par * 64
                            P.op('dve', 'tensor_tensor', out=hT[pb:pb + 64, 2 * j:2 * j + 2, cb:cb + 128],
                                 in0=ps[3][pb:pb + 64, :].rearrange("p (a u t) -> p a u t", a=2, u=2)[:, :, par, :],
                                 in1=wR[ri][pb:pb + 64, :].rearrange("p (a u t) -> p a u t", a=2, u=2)[:, :, par, :],
                                 op=ALU.mult, reads=['ps3', nR_ % ri], writes=['hT'])
                kmax = b0 + ncb
                for c in range(8 if 'b' in P2PARTS else 0):
                    ki = kv_i[0] % 2
                    kv_i[0] += 1
                    KTv = slabs[ki][:, 0:LP]
                    Vv = slabs[ki][:, LP:2 * LP].rearrange("p (b d) -> p b d", d=128)
                    kvn = 'slab%d' % ki
                    P.dma('sp', 'kt%d' % ki, out=KTv[:, 0:kmax * 128], in_=KTs[c, :, 0:kmax * 128], writes=[kvn])
                    for v0 in range(0, kmax, 8):
                        v1 = min(kmax, v0 + 8)
                        P.dma('sp', 'v%d_%d' % (ki, v0 // 8), out=Vv[:, v0:v1, :],
                              in_=Vs[v0 * 128:v1 * 128, c * 128:(c + 1) * 128].rearrange("(b p) d -> p b d", p=128),
                              writes=[kvn])
                    kbs = list(range(kmax - 1, -1, -1))
                    tiles = [(hh, kb) for kb in kbs for hh in range(2)]
                    nTl = len(tiles)

                    def mask_of(kb):
                        if b0 == 0:
                            return 5
                        if kb == 0:
                            return 4
                        if kb >= b0:
                            return kb - b0
                        return None

                    def stageA1(idx):
                        hh, kb = tiles[idx]
                        hb = hh * 64
                        w3 = idx % 3
                        pi = idx % 2
                        pn = 'ps%d' % pi
                        P.op('pe', 'matmul', ps[pi][:, 0:nt], lhsT=KTv[hb:hb + 64, kb * 128:(kb + 1) * 128],
                             rhs=Qs[hb:hb + 64, c, 0:nt], start=True, stop=True, reads=[kvn, 'Qs'], writes=[pn])
                        P.op('act', 'activation', out=wE[w3][:, 0:nt], in_=ps[pi][:, 0:nt], func=AF.Exp, scale=0.125,
                             reads=[pn], writes=['wE%d' % w3])

                    def stageA2(idx):
                        hh, kb = tiles[idx]
                        w3 = idx % 3
                        P.op('act', 'activation', out=wSP[w3][:, 0:nt], in_=wE[w3][:, 0:nt], func=AF.Ln, bias=1.0,
                             reads=['wE%d' % w3], writes=['wSP%d' % w3])
                        m = mask_of(kb)
                        if m is not None:
                            P.op('dve', 'tensor_tensor', out=wSP[w3][:, 0:nt], in0=wSP[w3][:, 0:nt],
                                 in1=sbm[:, m * 512:m * 512 + nt], op=ALU.mult,
                                 reads=['wSP%d' % w3, 'sbm'], writes=['wSP%d' % w3])
                            P.op('dve', 'tensor_tensor', out=wE[w3][:, 0:nt], in0=wE[w3][:, 0:nt],
                                 in1=sbm[:, m * 512:m * 512 + nt], op=ALU.mult,
                                 reads=['wE%d' % w3, 'sbm'], writes=['wE%d' % w3])

                    def stageB(idx):
                        hh, kb = tiles[idx]
                        w3 = idx % 3
                        w = idx % 2
                        first = (kb == kbs[0])
                        last = (kb == kbs[-1])
                        pg = 2 + w
                        P.op('pe', 'matmul', ps[pg][:, 0:nt], lhsT=TRI, rhs=wSP[w3][:, 0:nt], start=True, stop=first,
                             reads=['cbf', 'wSP%d' % w3], writes=['ps%d' % pg])
                        if not first:
                            P.op('pe', 'matmul', ps[pg][:, 0:nt], lhsT=ONESNEG, rhs=wACCb[hh][:, 0:nt], start=False, stop=True,
                                 reads=['cbf', 'wACCb%d' % hh], writes=['ps%d' % pg])
                        if idx >= 1:
                            stageAV(idx - 1)
                        if not last:
                            if first:
                                P.op('dve', 'tensor_copy', out=wACC[hh][:, 0:nt], in_=wSP[w3][:, 0:nt],
                                     reads=['wSP%d' % w3], writes=['wACC%d' % hh])
                            else:
                                P.op('dve', 'tensor_tensor', out=wACC[hh][:, 0:nt], in0=wACC[hh][:, 0:nt],
                                     in1=wSP[w3][:, 0:nt], op=ALU.add,
                                     reads=['wSP%d' % w3, 'wACC%d' % hh], writes=['wACC%d' % hh])
                            P.op('dve', 'tensor_copy', out=wACCb[hh][:, 0:nt], in_=wACC[hh][:, 0:nt],
                                 reads=['wACC%d' % hh], writes=['wACCb%d' % hh])
                        P.op('act', 'activation', out=wG[w][:, 0:nt], in_=ps[pg][:, 0:nt], func=AF.Exp,
                             reads=['ps%d' % pg], writes=['wG%d' % w])
                        P.op('dve', 'tensor_tensor', out=wA[w][:, 0:nt], in0=wE[w3][:, 0:nt],
                             in1=wG[w][:, 0:nt], op=ALU.mult,
                             reads=['wE%d' % w3, 'wG%d' % w], writes=['wA%d' % w])

                    def stageAV(idx):
                        hh, kb = tiles[idx]
                        w = idx % 2
                        first = (kb == kbs[0])
                        last = (kb == kbs[-1])
                        po = 4 + hh
                        P.op('pe', 'matmul', ps[po][:, 0:nt], lhsT=Vv[:, kb, :], rhs=wA[w][:, 0:nt], start=first, stop=last,
                             reads=[kvn, 'wA%d' % w], writes=['ps%d' % po])

                    stageA1(0)
                    if nTl > 1:
                        stageA1(1)
                    stageA2(0)
                    for idx in range(nTl):
                        if idx + 2 < nTl:
                            stageA1(idx + 2)
                        if idx + 1 < nTl:
                            stageA2(idx + 1)
                        stageB(idx)
                    stageAV(nTl - 1)
                    for hh in range(2):
                        pb = hh * 64
                        if hh == 0:
                            P.op('act', 'activation', out=hT[pb:pb + 64, 8 + c, 0:nt], in_=ps[4 + hh][pb:pb + 64, 0:nt],
                                 func=AF.Copy, reads=['ps%d' % (4 + hh)], writes=['hT'])
                        else:
                            P.op('dve', 'tensor_copy', out=hT[pb:pb + 64, 8 + c, 0:nt], in_=ps[4 + hh][pb:pb + 64, 0:nt],
                                 reads=['ps%d' % (4 + hh)], writes=['hT'])
                for grp in range(2 if 'n' in P2PARTS else 0):
                    norm_stats(hT[:, grp * 8:(grp + 1) * 8, :], 8, nt, ps[6], 'ps6', 'hT')
                    rstd_from_ps(ps[6], nt, 1.0 / 1024, wRS, 'wG0', 'ps6')
                    for k in range(8):
                        kc = grp * 8 + k
                        si = next_stg()
                        P.op('dve', 'scalar_tensor_tensor', out=stg[si][:, 0:nt], in0=hT[:, kc, 0:nt], scalar=gcol(l, 32 + kc),
                             in1=wRS[:, 0:nt], op0=ALU.mult, op1=ALU.mult, reads=['hT', 'wG0', 'pvec'], writes=['stg%d' % si])
                        P.dma('sp', 'stg%d' % si, out=MIXT[kc, :, t0:t0 + nt], in_=stg[si][:, 0:nt], reads=['stg%d' % si])

        def phase3(l):
            for (b0, ncb) in groups:
                t0 = b0 * 128
                nt = ncb * 128
                P.dma('sp', 'hT', out=hT[:, :, 0:nt], in_=HT[:, :, t0:t0 + nt].rearrange("k p t -> p k t"), writes=['hT'])
                P.dma('sp', 'xT', out=xT[:, :, 0:nt], in_=MIXT[:, :, t0:t0 + nt].rearrange("k p t -> p k t"), writes=['xT'])
                for s in range(4):
                    wv, wname = load_slab(w_o[l, :, s * 512:(s + 1) * 512], KC, 512)
                    for oc in range(4):
                        pi = next_ps4()
                        for kc in range(KC):
                            P.op('pe', 'matmul', ps[pi][:, 0:nt], lhsT=wv[:, kc, oc * 128:(oc + 1) * 128], rhs=xT[:, kc, 0:nt],
                                 start=(kc == 0), stop=(kc == KC - 1), reads=[wname, 'xT'], writes=['ps%d' % pi])
                        dc = s * 4 + oc
                        P.op('dve', 'tensor_tensor', out=hT[:, dc, 0:nt], in0=hT[:, dc, 0:nt], in1=ps[pi][:, 0:nt], op=ALU.add,
                             reads=['ps%d' % pi, 'hT'], writes=['hT'])
                big_rmsnorm(l, 16, nt)
                for s in range(11):
                    gv, gname = load_slab(w_gate[l, :, s * 512:(s + 1) * 512], KC, 512)
                    uv, uname = load_slab(w_up[l, :, s * 512:(s + 1) * 512], KC, 512)
                    for oc in range(4):
                        pg = next_ps4()
                        for kc in range(KC):
                            P.op('pe', 'matmul', ps[pg][:, 0:nt], lhsT=gv[:, kc, oc * 128:(oc + 1) * 128], rhs=xT[:, kc, 0:nt],
                                 start=(kc == 0), stop=(kc == KC - 1), reads=[gname, 'xT'], writes=['ps%d' % pg])
                        pu = next_ps4()
                        for kc in range(KC):
                            P.op('pe', 'matmul', ps[pu][:, 0:nt], lhsT=uv[:, kc, oc * 128:(oc + 1) * 128], rhs=xT[:, kc, 0:nt],
                                 start=(kc == 0), stop=(kc == KC - 1), reads=[uname, 'xT'], writes=['ps%d' % pu])
                        fc = s * 4 + oc
                        gi = fc % 2
                        P.op('act', 'activation', out=wSG[gi][:, 0:nt], in_=ps[pg][:, 0:nt], func=AF.Silu,
                             reads=['ps%d' % pg], writes=[nSG_ % gi])
                        P.op('dve', 'tensor_tensor', out=actT[:, fc, 0:nt], in0=wSG[gi][:, 0:nt], in1=ps[pu][:, 0:nt],
                             op=ALU.mult, reads=[nSG_ % gi, 'ps%d' % pu], writes=['actT'])
                for dcg in range(4):
                    for q in range(4):
                        dv, dname = load_slab(w_down[l, q * 11 * 128:(q + 1) * 11 * 128, dcg * 512:(dcg + 1) * 512], 11, 512)
                        for dcl in range(4):
                            for j in range(11):
                                P.op('pe', 'matmul', ps[4 + dcl][:, 0:nt], lhsT=dv[:, j, dcl * 128:(dcl + 1) * 128],
                                     rhs=actT[:, q * 11 + j, 0:nt], start=(q == 0 and j == 0), stop=(q == 3 and j == 10),
                                     reads=[dname, 'actT'], writes=['ps%d' % (4 + dcl)])
                    for dcl in range(4):
                        dc = dcg * 4 + dcl
                        P.op('dve', 'tensor_tensor', out=hT[:, dc, 0:nt], in0=hT[:, dc, 0:nt], in1=ps[4 + dcl][:, 0:nt],
                             op=ALU.add, reads=['ps%d' % (4 + dcl), 'hT'], writes=['hT'])
                P.dma('sp', 'hst', out=HT[:, :, t0:t0 + nt].rearrange("k p t -> p k t"), in_=hT[:, :, 0:nt], reads=['hT'])

        phase0()
        P.barrier()
        for l in range(DEPTH):
            if phases is None or 1 in phases:
                phase1(l)
                P.barrier()
            if phases is None or 2 in phases:
                phase2(l)
                P.barrier()
            if phases is None or 3 in phases:
                phase3(l)
                P.barrier(new_epoch=(l % 2 == 1 and l != DEPTH - 1))
        phaseF()
        P.barrier()
        P.emit()
        nops = {e: len(P.ops[e]) for e in ENGS}
        print("plan ops:", nops, "sems:", len(P.sems))
    return nc


def host_consts(NB):
    h = np.arange(1, 17, dtype=np.float32)
    slopes = np.exp2(-8.0 * h / 16).astype(np.float32)
    s = np.arange(128)[:, None].astype(np.float32)
    t = np.arange(128)[None, :].astype(np.float32)
    bias4 = np.full((128, 4, 16, 128), NEGB, np.float32)
    for hh in range(16):
        sl = slopes[hh]
        d = t - s
        bias4[:, 0, hh, :] = np.where(d >= 0, -sl * d, NEGB)
        d1 = t - s + 128
        bias4[:, 1, hh, :] = np.where(d1 < 128, -sl * d1, NEGB)
        bias4[:, 2, hh, :] = np.where(s >= 112, -sl * d, NEGB)
        bias4[:, 3, hh, :] = np.where((s >= 112) & (d >= 0), -sl * d, NEGB)
    ctab = np.zeros((128, NB, 16), np.float32)
    for b in range(NB):
        ctab[:, b, :] = -slopes[None, :] * 128.0 * b
    sbm = np.zeros((128, 6, 4, 128), np.float32)
    for m in range(4):
        for c in range(4):
            if c > m:
                sbm[:, m, c, :] = 1.0
            elif c == m:
                sbm[:, m, c, :] = (s < t)
    sbm[112:, 4, :, :] = 1.0
    sbm[:, 5, 0, :] = ((s >= 112) & (s < t))
    j = np.arange(128)[:, None]
    si = np.arange(128)[None, :]
    cbf = np.zeros((128, 3, 128), np.float32)
    cbf[:, 0, :] = -(j >= si).astype(np.float32)
    cbf[:, 1, :] = -1.0
    cbf[:, 2, :] = 1.0
    cf32 = np.zeros((128, 3, 128), np.float32)
    cf32[:, 0, :] = 1.0
    cf32[:, 1, :] = ((j // 64) == (si // 64)).astype(np.float32)
    cf32[:, 2, :] = np.eye(128, dtype=np.float32)
    return dict(bias4=bias4.reshape(128, -1), ctab=ctab.reshape(128, -1), sbm=sbm.reshape(128, -1),
                cbf=cbf.reshape(128, -1), cf32=cf32.reshape(128, -1))


def make_inputs(NB, DEPTH, x, meta_tokens, attn_norm_g, w_in, q_norm_g, k_norm_g, attn_sinks,
                swa_out_g, sb_out_g, w_o, ffn_norm_g, w_gate, w_up, w_down):
    consts = host_consts(NB)
    f = lambda a: np.ascontiguousarray(np.asarray(a, dtype=np.float32))
    B = x.shape[0]
    pvec = np.zeros((128, DEPTH * NPV), np.float32)
    for l in range(DEPTH):
        o = l * NPV
        pvec[:, o:o + 16] = f(attn_norm_g[l]).reshape(16, 128).T
        pvec[:, o + 16:o + 32] = f(ffn_norm_g[l]).reshape(16, 128).T
        pvec[:, o + 32:o + 40] = f(swa_out_g[l]).reshape(8, 128).T
        pvec[:, o + 40:o + 48] = f(sb_out_g[l]).reshape(8, 128).T
        pvec[:, o + 48] = np.tile(f(q_norm_g[l]), 2)
        pvec[:, o + 49] = np.tile(f(k_norm_g[l]), 2)
    sinks = f(attn_sinks).reshape(1, DEPTH * 16)
    shared = dict(w_in=f(w_in), w_o=f(w_o), w_gate=f(w_gate), w_up=f(w_up), w_down=f(w_down),
                  pvec=pvec, sinks=sinks, **consts)
    in_maps = []
    xs = {}
    for core in range(8):
        b = core % B
        if b not in xs:
            xi = np.zeros((NB * 128, D), np.float32)
            xi[112:128] = f(meta_tokens)
            xi[128:] = f(x[b])
            xs[b] = xi
        m = dict(shared)
        m["xin"] = xs[b]
        in_maps.append(m)
    return in_maps


_NC_CACHE = {}
PHASES = None
DEBUG = False
import os as _os
P2PARTS = _os.environ.get('P2PARTS', 'abn12345')


def kernel(x, meta_tokens, attn_norm_g, w_in, q_norm_g, k_norm_g, attn_sinks,
           swa_out_g, sb_out_g, w_o, ffn_norm_g, w_gate, w_up, w_down):
    x = np.asarray(x)
    B, S, _ = x.shape
    NB = S // 128 + 1
    DEPTH = np.asarray(w_in).shape[0]
    key = (NB, DEPTH)
    if key not in _NC_CACHE:
        _NC_CACHE[key] = build(NB, DEPTH, debug=DEBUG, phases=PHASES)
    nc = _NC_CACHE[key]
    in_maps = make_inputs(NB, DEPTH, x, meta_tokens, attn_norm_g, w_in, q_norm_g, k_norm_g, attn_sinks,
                          swa_out_g, sb_out_g, w_o, ffn_norm_g, w_gate, w_up, w_down)
    res = run_bass_kernel_spmd(nc, in_maps, core_ids=list(range(8)))
    out = np.stack([np.asarray(res.results[b]["out"], dtype=np.float32) for b in range(B)], axis=0)
    if DEBUG:
        global LAST_RES
        LAST_RES = res.results
    return out
```

```python
import numpy as np
from contextlib import ExitStack
import concourse.bass as bass
import concourse.mybir as mybir
from concourse.bass_utils import run_bass_kernel_spmd

F32 = mybir.dt.float32
BF16 = mybir.dt.bfloat16
AF = mybir.ActivationFunctionType
ALU = mybir.AluOpType

D = 2048
KC = 16
PW = 4608
DFF = 5632
FC = 44
EPS = 1e-6
NEGB = -30000.0
NPV = 50
ENGS = ['pe', 'act', 'dve', 'pool', 'sp']
SAME_ENGINE_SYNC = True


class Plan:
    def __init__(self, nc, stack):
        self.nc = nc
        self.stack = stack
        self.ops = {e: [] for e in ENGS}
        self.last_w = {}
        self.readers = {}
        self.known = {e: {} for e in ENGS}
        self.epoch = 0
        self.sems = {}
        self.dma_cnt = {}

    def _sem(self, key):
        k = (key, self.epoch)
        if k not in self.sems:
            self.sems[k] = self.stack.enter_context(
                self.nc.semaphore("s%d_%s" % (self.epoch, key.replace(':', '_'))))
        return k

    def _deps(self, eng, reads, writes):
        deps = []
        for r in reads:
            d = self.last_w.get(r)
            if d is not None:
                deps.append(d)
        for w in writes:
            d = self.last_w.get(w)
            if d is not None:
                deps.append(d)
            deps.extend(self.readers.get(w, ()))
        waits = {}
        for (key, n) in deps:
            if key == eng and (eng == 'pe' or not SAME_ENGINE_SYNC):
                continue
            if self.known[eng].get(key, 0) >= n:
                continue
            if waits.get(key, 0) < n:
                waits[key] = n
        for key, n in waits.items():
            self.known[eng][key] = n
        return waits

    def _mark(self, me, reads, writes):
        for r in reads:
            self.readers.setdefault(r, []).append(me)
        for w in writes:
            self.last_w[w] = me
            self.readers[w] = []

    def op(self, eng, meth, *args, reads=(), writes=(), **kw):
        fn = (meth, args, kw)
        waits = self._deps(eng, reads, writes)
        semk = self._sem(eng)
        idx = len(self.ops[eng])
        rec = dict(kind='op', fn=fn, waits=[(self._sem(k), k, n) for k, n in waits.items()],
                   sem=semk, sig=False, idx=idx)
        self.ops[eng].append(rec)
        self._mark((eng, idx + 1), reads, writes)

    def dma(self, eng, semname, reads=(), writes=(), **kw):
        fn = ('dma_start', (), kw)
        key = 'dma:' + semname
        n_prev = self.dma_cnt.get((key, self.epoch), 0)
        waits = self._deps(eng, reads, writes)
        if n_prev > 0 and self.known[eng].get(key, 0) < n_prev:
            waits[key] = max(waits.get(key, 0), n_prev)
            self.known[eng][key] = n_prev
        semk = self._sem(key)
        n = n_prev + 1
        self.dma_cnt[(key, self.epoch)] = n
        rec = dict(kind='dma', fn=fn, waits=[(self._sem(k), k, m) for k, m in waits.items()],
                   sem=semk, idx=len(self.ops[eng]))
        self.ops[eng].append(rec)
        self._mark((key, n), reads, writes)

    def barrier(self, new_epoch=False):
        finals = []
        for e in ENGS:
            for rec in reversed(self.ops[e]):
                if rec['kind'] == 'op':
                    if rec['sem'][1] == self.epoch:
                        finals.append((e, rec['idx'] + 1))
                    break
        for (key, ep), n in self.dma_cnt.items():
            if ep == self.epoch:
                finals.append((key, n))
        for e in ENGS:
            waits = [(self._sem(key), key, n) for key, n in finals if key != e]
            self.ops[e].append(dict(kind='wait', waits=waits, idx=len(self.ops[e])))
        self.last_w = {}
        self.readers = {}
        if new_epoch:
            self.epoch += 1
            self.known = {e: {} for e in ENGS}
        else:
            for e in ENGS:
                for key, n in finals:
                    if key != e:
                        self.known[e][key] = max(self.known[e].get(key, 0), n)

    def emit(self):
        nc = self.nc
        for e in ENGS:
            for rec in self.ops[e]:
                for (semk, key, n) in rec['waits']:
                    if not key.startswith('dma:'):
                        self.ops[key][n - 1]['sig'] = True
        val = {}
        for e in ENGS:
            c = {}
            for rec in self.ops[e]:
                if rec['kind'] == 'op' and rec['sig']:
                    ep = rec['sem'][1]
                    c[ep] = c.get(ep, 0) + 1
                    val[(e, rec['idx'] + 1)] = c[ep]
        engobj = {'pe': 'tensor', 'act': 'scalar', 'dve': 'vector', 'pool': 'gpsimd', 'sp': 'sync'}
        with nc.Block() as block:
            for e in ENGS:
                def body(eng, e=e):
                    for rec in self.ops[e]:
                        for (semk, key, n) in rec['waits']:
                            v = 16 * n if key.startswith('dma:') else val[(key, n)]
                            eng.wait_ge(self.sems[semk], v)
                        if rec['kind'] == 'op':
                            m, a, k = rec['fn']
                            ins = getattr(eng, m)(*a, **k)
                            if rec['sig']:
                                ins.then_inc(self.sems[rec['sem']], 1)
                        elif rec['kind'] == 'dma':
                            m, a, k = rec['fn']
                            getattr(eng, m)(*a, **k).then_inc(self.sems[rec['sem']], 16)
                getattr(block, engobj[e])(body)


def groups_of(NB):
    gs = [(0, 1)]
    b = 1
    while b < NB:
        n = min(4, NB - b)
        gs.append((b, n))
        b += n
    return gs


def build(NB, DEPTH, debug=False, phases=None):
    LP = NB * 128
    nc = bass.Bass("TRN2", target_bir_lowering=False)
    dt_in = lambda name, shape: nc.dram_tensor(name, shape, F32, kind="ExternalInput").ap()
    xin = dt_in("xin", [LP, D])
    w_in = dt_in("w_in", [DEPTH, D, PW])
    w_o = dt_in("w_o", [DEPTH, D, D])
    w_gate = dt_in("w_gate", [DEPTH, D, DFF])
    w_up = dt_in("w_up", [DEPTH, D, DFF])
    w_down = dt_in("w_down", [DEPTH, DFF, D])
    pvec_d = dt_in("pvec", [128, DEPTH * NPV])
    sinks_d = dt_in("sinks", [1, DEPTH * 16])
    bias4_d = dt_in("bias4", [128, 4 * 16 * 128])
    ctab_d = dt_in("ctab", [128, NB * 16])
    sbm_d = dt_in("sbm", [128, 6 * 512])
    cbf_d = dt_in("cbf", [128, 3 * 128])
    cf32_d = dt_in("cf32", [128, 3 * 128])
    out_d = nc.dram_tensor("out", [LP - 128, D], F32, kind="ExternalOutput").ap()
    skind = "ExternalOutput" if debug else "Internal"
    scr = lambda name, shape, dt: nc.dram_tensor(name, shape, dt, kind=skind).ap()
    HT = scr("HT", [KC, 128, LP], F32)
    QTs = scr("QTs", [8, 128, LP], BF16)
    KTs = scr("KTs", [8, 128, LP], BF16)
    Vs = scr("Vs", [LP, 1024], BF16)
    QTa = scr("QTa", [8, 128, LP], BF16)
    KTa = scr("KTa", [4, 64, LP], BF16)
    Va = scr("Va", [LP, 256], BF16)
    MIXT = scr("MIXT", [KC, 128, LP], BF16)
    WB_in = scr("WB_in", [DEPTH, D, PW], BF16)
    WB_o = scr("WB_o", [DEPTH, D, D], BF16)
    WB_g = scr("WB_g", [DEPTH, D, DFF], BF16)
    WB_u = scr("WB_u", [DEPTH, D, DFF], BF16)
    WB_d = scr("WB_d", [DEPTH, DFF, D], BF16)

    groups = groups_of(NB)

    with ExitStack() as st:
        sb = lambda name, shape, dt: st.enter_context(nc.sbuf_tensor("s_" + name, shape, dt))
        B_h = sb("B_h", [128, KC * 512], F32)
        B_x = sb("B_x", [128, KC * 512], BF16)
        NSL = 3
        SLW = max(8704, 2 * LP)
        slabs = [sb("slab%d" % i, [128, SLW], BF16) for i in range(NSL)]
        B_act = sb("B_act", [128, FC * 512], BF16)
        pvec = sb("pvec", [128, DEPTH * NPV], F32)
        esink = sb("esink", [128, DEPTH * 16], F32)
        ctab = sb("ctab", [128, NB * 16], F32)
        sbm = sb("sbm", [128, 6 * 512], BF16)
        cbf = sb("cbf", [128, 3 * 128], BF16)
        cf32 = sb("cf32", [128, 3 * 128], F32)
        wE = [sb("wE%d" % i, [128, 512], F32) for i in range(3)]
        wSP = [sb("wSP%d" % i, [128, 512], BF16) for i in range(3)]
        wG = [sb("wG%d" % i, [128, 512], F32) for i in range(2)]
        wA = [sb("wA%d" % i, [128, 512], BF16) for i in range(3)]
        wACC = [sb("wACC%d" % i, [128, 512], F32) for i in range(2)]
        wACCb = [sb("wACCb%d" % i, [128, 512], BF16) for i in range(2)]
        stg = [sb("stg%d" % i, [128, 512], BF16) for i in range(4)]
        wRS = wG[0]
        wT, nT_ = wG, 'wG%d'
        wR, nR_ = wE, 'wE%d'
        wSQ, nSQ_ = wACC, 'wACC%d'
        wRS2, nRS2_ = wG, 'wG%d'
        wSG, nSG_ = wE, 'wE%d'
        wPT = [wSP[0], wSP[1], wA[0]]
        nPT = ['wSP0', 'wSP1', 'wA0']
        ps = [st.enter_context(nc.psum_tensor("ps%d" % i, [128, 512], F32)) for i in range(8)]

        P = Plan(nc, st)
        TRI = cbf[:, 0:128]
        ONESNEG = cbf[:, 128:256]
        ONESB = cbf[:, 256:384]
        ONESF = cf32[:, 0:128]
        BDF = cf32[:, 128:256]
        IDF = cf32[:, 256:384]

        hT = B_h[:, :].rearrange("p (k t) -> p k t", k=KC)
        xT = B_x[:, :].rearrange("p (k t) -> p k t", k=KC)
        actT = B_act[:, :].rearrange("p (k t) -> p k t", k=FC)
        bias4 = B_act[:, 0:16384].bitcast(F32).rearrange("p (s h t) -> p s h t", s=4, h=16)
        KA = B_act[:, 16384:16384 + 3072].rearrange("p (j t) -> p j t", j=4)
        VA = B_act[:, 19456:19456 + 3072].rearrange("p (b j d) -> p b j d", b=6, j=4)
        Qs = B_x[:, 0:4096].rearrange("p (c t) -> p c t", c=8)
        Qa16 = slabs[2][:, 0:8192].rearrange("p (h t) -> p h t", h=16)
        tokbuf = B_act[:, 0:4096].bitcast(F32)

        stg_i = [0]

        def next_stg():
            i = stg_i[0] % 4
            stg_i[0] += 1
            return i

        slab_i = [0]

        def load_slab(src_ap, nk, ncols):
            i = slab_i[0] % NSL
            slab_i[0] += 1
            view = slabs[i][:, 0:nk * ncols].rearrange("p (k n) -> p k n", k=nk)
            src = src_ap.rearrange("(k p) n -> p k n", p=128)
            P.dma('pool', 'slab%d' % i, out=view, in_=src, reads=['WB%d' % wl[0]], writes=['slab%d' % i])
            return view, 'slab%d' % i

        wl = [0]

        cv_i = [0]

        def convert_weights(l):
            for (src, dst, rows) in ((w_in, WB_in, D), (w_o, WB_o, D), (w_gate, WB_g, D), (w_up, WB_u, D), (w_down, WB_d, DFF)):
                for r0 in range(0, rows, 512):
                    r1 = min(rows, r0 + 512)
                    i = cv_i[0] % 4
                    cv_i[0] += 1
                    P.dma('pool', 'cv%d' % i, out=dst[l, r0:r1, :], in_=src[l, r0:r1, :], writes=['WB%d' % l])

        P.dma('sp', 'c0', out=pvec[:, :], in_=pvec_d[:, :], writes=['pvec'])
        P.dma('sp', 'c1', out=ctab[:, :], in_=ctab_d[:, :], writes=['ctab'])
        P.dma('sp', 'c2', out=cf32[:, :], in_=cf32_d[:, :], writes=['cf32'])
        P.dma('sp', 'c3', out=esink[:, :], in_=sinks_d[:, :].broadcast_to([128, DEPTH * 16]), writes=['esink'])
        P.dma('pool', 'c4', out=sbm[:, :], in_=sbm_d[:, :], writes=['sbm'])
        P.dma('pool', 'c5', out=cbf[:, :], in_=cbf_d[:, :], writes=['cbf'])
        P.op('act', 'activation', out=esink[:, :], in_=esink[:, :], func=AF.Exp, reads=['esink'], writes=['esink'])

        qsc = sb("qsc", [128, DEPTH], F32)
        for l_ in range(DEPTH):
            P.op('act', 'activation', out=qsc[:, l_:l_ + 1], in_=pvec[:, l_ * NPV + 48:l_ * NPV + 49], func=AF.Copy,
                 scale=0.125, reads=['pvec'], writes=['qsc'])

        def gcol(l, j):
            return pvec[:, l * NPV + j: l * NPV + j + 1]

        def rstd_from_ps(psb, nt, inv_n, out_tile, wname, psname):
            P.op('act', 'activation', out=out_tile[:, 0:nt], in_=psb[:, 0:nt], func=AF.Sqrt, scale=inv_n, bias=EPS,
                 reads=[psname], writes=[wname])
            P.op('dve', 'reciprocal', out=out_tile[:, 0:nt], in_=out_tile[:, 0:nt], reads=[wname], writes=[wname])

        sq_i = [0]

        def norm_stats(src_view, nk, nt, psb, psname, src_name):
            for k in range(nk):
                i = sq_i[0] % 2
                sq_i[0] += 1
                P.op('act', 'activation', out=wSQ[i][:, 0:nt], in_=src_view[:, k, 0:nt], func=AF.Square,
                     reads=[src_name], writes=[nSQ_ % i])
                P.op('pe', 'matmul', psb[:, 0:nt], lhsT=ONESF, rhs=wSQ[i][:, 0:nt], start=(k == 0), stop=(k == nk - 1),
                     reads=[nSQ_ % i, 'cf32'], writes=[psname])

        def big_rmsnorm(l, gbase, nt):
            norm_stats(hT, KC, nt, ps[6], 'ps6', 'hT')
            rstd_from_ps(ps[6], nt, 1.0 / D, wRS, 'wG0', 'ps6')
            for kc in range(KC):
                P.op('dve', 'scalar_tensor_tensor', out=xT[:, kc, 0:nt], in0=hT[:, kc, 0:nt], scalar=gcol(l, gbase + kc),
                     in1=wRS[:, 0:nt], op0=ALU.mult, op1=ALU.mult, reads=['hT', 'wG0', 'pvec'], writes=['xT'])

        def store_chunk(dst_ap, src_psb, psname, nt):
            i = next_stg()
            P.op('act', 'activation', out=stg[i][:, 0:nt], in_=src_psb[:, 0:nt], func=AF.Copy,
                 reads=[psname], writes=['stg%d' % i])
            P.dma('sp', 'stg%d' % i, out=dst_ap, in_=stg[i][:, 0:nt], reads=['stg%d' % i])

        psr = [0]

        def next_ps4():
            i = psr[0] % 4
            psr[0] += 1
            return i

        def phase0():
            for b in range(NB):
                P.dma('sp', 'tok', out=tokbuf, in_=xin[b * 128:(b + 1) * 128, :], writes=['tokbuf'])
                for q in range(4):
                    pi = next_ps4()
                    for j in range(4):
                        kc = q * 4 + j
                        P.op('pe', 'transpose', ps[pi][:, j * 128:(j + 1) * 128], tokbuf[:, kc * 128:(kc + 1) * 128], IDF,
                             reads=['tokbuf', 'cf32'], writes=['ps%d' % pi])
                    src = ps[pi][:, :].rearrange("p (a t) -> p a t", a=4)
                    dst = hT[:, q * 4:(q + 1) * 4, 0:128]
                    if q % 2 == 0:
                        P.op('act', 'activation', out=dst, in_=src, func=AF.Copy, reads=['ps%d' % pi], writes=['hT'])
                    else:
                        P.op('dve', 'tensor_copy', out=dst, in_=src, reads=['ps%d' % pi], writes=['hT'])
                P.dma('sp', 'hst', out=HT[:, :, b * 128:(b + 1) * 128].rearrange("k p t -> p k t"), in_=hT[:, :, 0:128],
                      reads=['hT'])

        def phaseF():
            tb4 = tokbuf.rearrange("p (q n) -> p q n", q=4)
            for b in range(1, NB):
                P.dma('sp', 'hT', out=hT[:, :, 0:128], in_=HT[:, :, b * 128:(b + 1) * 128].rearrange("k p t -> p k t"),
                      writes=['hT'])
                for q in range(4):
                    pi = next_ps4()
                    for j in range(4):
                        kc = q * 4 + j
                        P.op('pe', 'transpose', ps[pi][:, j * 128:(j + 1) * 128], hT[:, kc, 0:128], IDF,
                             reads=['hT', 'cf32'], writes=['ps%d' % pi])
                    if q % 2 == 0:
                        P.op('act', 'activation', out=tb4[:, q, :], in_=ps[pi][:, :], func=AF.Copy,
                             reads=['ps%d' % pi], writes=['tokbuf'])
                    else:
                        P.op('dve', 'tensor_copy', out=tb4[:, q, :], in_=ps[pi][:, :], reads=['ps%d' % pi], writes=['tokbuf'])
                P.dma('sp', 'tok', out=out_d[(b - 1) * 128:b * 128, :], in_=tokbuf, reads=['tokbuf'])

        def phase1(l):
            for (b0, ncb) in groups:
                t0 = b0 * 128
                nt = ncb * 128
                P.dma('sp', 'hT', out=hT[:, :, 0:nt], in_=HT[:, :, t0:t0 + nt].rearrange("k p t -> p k t"), writes=['hT'])
                big_rmsnorm(l, 0, nt)
                for s in range(9):
                    wv, wname = load_slab(WB_in[l, :, s * 512:(s + 1) * 512], KC, 512)
                    if s <= 6:
                        noc = 4 if s != 2 else 2
                        for oc in range(noc):
                            pi = next_ps4()
                            pn = 'ps%d' % pi
                            for kc in range(KC):
                                P.op('pe', 'matmul', ps[pi][:, 0:nt], lhsT=wv[:, kc, oc * 128:(oc + 1) * 128],
                                     rhs=xT[:, kc, 0:nt], start=(kc == 0), stop=(kc == KC - 1),
                                     reads=[wname, 'xT'], writes=[pn])
                            if s in (3, 4):
                                store_chunk(QTs[(s - 3) * 4 + oc, :, t0:t0 + nt], ps[pi], pn, nt)
                            elif s in (5, 6):
                                store_chunk(KTs[(s - 5) * 4 + oc, :, t0:t0 + nt], ps[pi], pn, nt)
                            else:
                                i = sq_i[0] % 2
                                sq_i[0] += 1
                                pj = 4 + i
                                pjn = 'ps%d' % pj
                                P.op('act', 'activation', out=wSQ[i][:, 0:nt], in_=ps[pi][:, 0:nt], func=AF.Square,
                                     reads=[pn], writes=[nSQ_ % i])
                                P.op('pe', 'matmul', ps[pj][:, 0:nt], lhsT=BDF, rhs=wSQ[i][:, 0:nt], start=True, stop=True,
                                     reads=[nSQ_ % i, 'cf32'], writes=[pjn])
                                rstd_from_ps(ps[pj], nt, 1.0 / 64, wRS2[i], nRS2_ % i, pjn)
                                si = next_stg()
                                gc = qsc[:, l:l + 1] if s < 2 else gcol(l, 49)
                                P.op('dve', 'scalar_tensor_tensor', out=stg[si][:, 0:nt], in0=ps[pi][:, 0:nt], scalar=gc,
                                     in1=wRS2[i][:, 0:nt], op0=ALU.mult, op1=ALU.mult,
                                     reads=[pn, nRS2_ % i, 'pvec', 'qsc'], writes=['stg%d' % si])
                                if s < 2:
                                    P.dma('sp', 'stg%d' % si, out=QTa[s * 4 + oc, :, t0:t0 + nt], in_=stg[si][:, 0:nt],
                                          reads=['stg%d' % si])
                                else:
                                    P.dma('sp', 'stg%d' % si,
                                          out=KTa[2 * oc:2 * oc + 2, :, t0:t0 + nt].rearrange("j p t -> (j p) t"),
                                          in_=stg[si][:, 0:nt], reads=['stg%d' % si])
                    if s in (2, 7, 8):
                        c0, ncol = (256, 256) if s == 2 else (0, 512)
                        for tb in range(ncb):
                            pi = next_ps4()
                            pn = 'ps%d' % pi
                            for kc in range(KC):
                                P.op('pe', 'matmul', ps[pi][:, 0:ncol], lhsT=xT[:, kc, tb * 128:(tb + 1) * 128],
                                     rhs=wv[:, kc, c0:c0 + ncol], start=(kc == 0), stop=(kc == KC - 1),
                                     reads=[wname, 'xT'], writes=[pn])
                            r0 = t0 + tb * 128
                            if s == 2:
                                store_chunk(Va[r0:r0 + 128, :], ps[pi], pn, 256)
                            else:
                                store_chunk(Vs[r0:r0 + 128, (s - 7) * 512:(s - 6) * 512], ps[pi], pn, 512)

        def phase2(l):
            P.dma('sp', 'bias', out=B_act[:, 0:16384].bitcast(F32), in_=bias4_d[:, :], writes=['bias4'])
            kv_i = [0]
            tile_i = [0]
            for (b0, ncb) in groups:
                t0 = b0 * 128
                nt = ncb * 128
                P.dma('sp', 'qs', out=Qs[:, :, 0:nt], in_=QTs[:, :, t0:t0 + nt].rearrange("c p t -> p c t"), writes=['Qs'])
                P.dma('sp', 'qa', out=Qa16[0:64, :, 0:nt],
                      in_=QTa.rearrange("c (two p) t -> (c two) p t", two=2)[:, :, t0:t0 + nt].rearrange("h p t -> p h t"),
                      writes=['slab2'])
                kb_lo = max(b0 - 1, 0)
                nkb = b0 + ncb - kb_lo
                P.dma('sp', 'ka0', out=KA[0:64, :, 0:128], in_=KTa[:, :, 0:128].rearrange("j p t -> p j t"),
                      writes=['KA'])
                P.dma('sp', 'ka0', out=KA[0:64, :, 128:128 + nkb * 128],
                      in_=KTa[:, :, kb_lo * 128:(kb_lo + nkb) * 128].rearrange("j p t -> p j t"), writes=['KA'])
                for half in range(2):
                    hs = slice(half * 64, (half + 1) * 64)
                    P.dma('sp', 'va%d' % half, out=VA[:, 0, :, hs], in_=Va[0:128, :].rearrange("p (j d) -> p j d", j=4),
                          writes=['VA'])
                    for kk in range(nkb):
                        P.dma('sp', 'va%d' % half, out=VA[:, 1 + kk, :, hs],
                              in_=Va[(kb_lo + kk) * 128:(kb_lo + kk + 1) * 128, :].rearrange("p (j d) -> p j d", j=4),
                              writes=['VA'])
                for bi in range(ncb if 'a' in P2PARTS else 0):
                    b = b0 + bi
                    cb = bi * 128
                    if b == 0:
                        segs = [(3, 0, 0)]
                    else:
                        segs = []
                        if b >= 2:
                            segs.append((1, 128 + (b - 1 - kb_lo) * 128, 1 + (b - 1 - kb_lo)))
                        segs.append((0, 128 + (b - kb_lo) * 128, 1 + (b - kb_lo)))
                        segs.append((2, 0, 0))
                    ns = len(segs)
                    for j in range(4):
                        for si, (bidx, kcol, vblk) in enumerate(segs):
                            pi = si
                            pn = 'ps%d' % pi
                            for u in range(4):
                                h = 4 * j + u
                                hb = (u % 2) * 64
                                P.op('pe', 'matmul', ps[pi][:, u * 128:(u + 1) * 128], lhsT=KA[0:64, j, kcol:kcol + 128],
                                     rhs=Qa16[0:64, h, cb:cb + 128], start=True, stop=True,
                                     reads=['KA', 'slab2'], writes=[pn])
                            ti = tile_i[0] % 2
                            tile_i[0] += 1
                            tv = wT[ti][:, :].rearrange("p (u t) -> p u t", u=4)
                            if '4' not in P2PARTS:
                                continue
                            P.op('dve', 'tensor_tensor', out=tv, in0=ps[pi][:, :].rearrange("p (u t) -> p u t", u=4),
                                 in1=bias4[:, bidx, 4 * j:4 * j + 4, :], op=ALU.add,
                                 reads=[pn, 'bias4'], writes=[nT_ % ti])
                            if bidx == 2 and '1' in P2PARTS:
                                P.op('dve', 'tensor_tensor', out=tv, in0=tv,
                                     in1=ctab[:, b * 16 + 4 * j:b * 16 + 4 * j + 4].unsqueeze(2).broadcast_to([128, 4, 128]),
                                     op=ALU.add, reads=[nT_ % ti, 'ctab'], writes=[nT_ % ti])
                            if '5' in P2PARTS:
                                P.op('act', 'activation', out=wPT[si][:, :], in_=wT[ti][:, :], func=AF.Exp,
                                     reads=[nT_ % ti], writes=[nPT[si]])
                        for u in range(4 if '2' in P2PARTS else 0):
                            for si, (bidx, kcol, vblk) in enumerate(segs):
                                P.op('pe', 'matmul', ps[3][:, u * 128:(u + 1) * 128], lhsT=VA[:, vblk, j, :],
                                     rhs=wPT[si][:, u * 128:(u + 1) * 128], start=(si == 0), stop=(si == ns - 1),
                                     reads=['VA', nPT[si]], writes=['ps3'])
                        for u in range(4 if '2' in P2PARTS else 0):
                            for si, (bidx, kcol, vblk) in enumerate(segs):
                                P.op('pe', 'matmul', ps[4][:, u * 128:(u + 1) * 128], lhsT=ONESB,
                                     rhs=wPT[si][:, u * 128:(u + 1) * 128], start=(si == 0), stop=(si == ns - 1),
                                     reads=['cbf', nPT[si]], writes=['ps4'])
                        if '3' not in P2PARTS:
                            continue
                        ri = tile_i[0] % 2
                        rv = wR[ri][:, :].rearrange("p (u t) -> p u t", u=4)
                        P.op('dve', 'tensor_tensor', out=rv, in0=ps[4][:, :].rearrange("p (u t) -> p u t", u=4),
                             in1=esink[:, l * 16 + 4 * j:l * 16 + 4 * j + 4].unsqueeze(2).broadcast_to([128, 4, 128]),
                             op=ALU.add, reads=['ps4', 'esink'], writes=[nR_ % ri])
                        P.op('dve', 'reciprocal', out=wR[ri][:, :], in_=wR[ri][:, :], reads=[nR_ % ri], writes=[nR_ % ri])
                        for par in range(2):
                            pb = par * 64
                            P.op('dve', 'tensor_tensor', out=hT[pb:pb + 64, 2 * j:2 * j + 2, cb:cb + 128],
                                 in0=ps[3][pb:pb + 64, :].rearrange("p (a u t) -> p a u t", a=2, u=2)[:, :, par, :],
                                 in1=wR[ri][pb:pb + 64, :].rearrange("p (a u t) -> p a u t", a=2, u=2)[:, :, par, :],
                                 op=ALU.mult, reads=['ps3', nR_ % ri], writes=['hT'])
                kmax = b0 + ncb
                for c in range(8 if 'b' in P2PARTS else 0):
                    ki = kv_i[0] % 2
                    kv_i[0] += 1
                    KTv = slabs[ki][:, 0:LP]
                    Vv = slabs[ki][:, LP:2 * LP].rearrange("p (b d) -> p b d", d=128)
                    kvn = 'slab%d' % ki
                    P.dma('sp', 'kt%d' % ki, out=KTv[:, 0:kmax * 128], in_=KTs[c, :, 0:kmax * 128], writes=[kvn])
                    for v0 in range(0, kmax, 8):
                        v1 = min(kmax, v0 + 8)
                        P.dma('sp', 'v%d_%d' % (ki, v0 // 8), out=Vv[:, v0:v1, :],
                              in_=Vs[v0 * 128:v1 * 128, c * 128:(c + 1) * 128].rearrange("(b p) d -> p b d", p=128),
                              writes=[kvn])
                    kbs = list(range(kmax - 1, -1, -1))
                    tiles = [(hh, kb) for kb in kbs for hh in range(2)]
                    nTl = len(tiles)

                    def mask_of(kb):
                        if b0 == 0:
                            return 5
                        if kb == 0:
                            return 4
                        if kb >= b0:
                            return kb - b0
                        return None

                    def stageA1(idx):
                        hh, kb = tiles[idx]
                        hb = hh * 64
                        w3 = idx % 3
                        pi = idx % 2
                        pn = 'ps%d' % pi
                        P.op('pe', 'matmul', ps[pi][:, 0:nt], lhsT=KTv[hb:hb + 64, kb * 128:(kb + 1) * 128],
                             rhs=Qs[hb:hb + 64, c, 0:nt], start=True, stop=True, reads=[kvn, 'Qs'], writes=[pn])
                        P.op('act', 'activation', out=wE[w3][:, 0:nt], in_=ps[pi][:, 0:nt], func=AF.Exp, scale=0.125,
                             reads=[pn], writes=['wE%d' % w3])

                    def stageA2(idx):
                        hh, kb = tiles[idx]
                        w3 = idx % 3
                        P.op('act', 'activation', out=wSP[w3][:, 0:nt], in_=wE[w3][:, 0:nt], func=AF.Ln, bias=1.0,
                             reads=['wE%d' % w3], writes=['wSP%d' % w3])
                        m = mask_of(kb)
                        if m is not None:
                            P.op('dve', 'tensor_tensor', out=wSP[w3][:, 0:nt], in0=wSP[w3][:, 0:nt],
                                 in1=sbm[:, m * 512:m * 512 + nt], op=ALU.mult,
                                 reads=['wSP%d' % w3, 'sbm'], writes=['wSP%d' % w3])
                            P.op('dve', 'tensor_tensor', out=wE[w3][:, 0:nt], in0=wE[w3][:, 0:nt],
                                 in1=sbm[:, m * 512:m * 512 + nt], op=ALU.mult,
                                 reads=['wE%d' % w3, 'sbm'], writes=['wE%d' % w3])

                    def stageB(idx):
                        hh, kb = tiles[idx]
                        w3 = idx % 3
                        w = idx % 2
                        first = (kb == kbs[0])
                        last = (kb == kbs[-1])
                        pg = 2 + w
                        P.op('pe', 'matmul', ps[pg][:, 0:nt], lhsT=TRI, rhs=wSP[w3][:, 0:nt], start=True, stop=first,
                             reads=['cbf', 'wSP%d' % w3], writes=['ps%d' % pg])
                        if not first:
                            P.op('pe', 'matmul', ps[pg][:, 0:nt], lhsT=ONESNEG, rhs=wACCb[hh][:, 0:nt], start=False, stop=True,
                                 reads=['cbf', 'wACCb%d' % hh], writes=['ps%d' % pg])
                        if idx >= 2:
                            stageAV(idx - 2)
                        if not last:
                            if first:
                                P.op('dve', 'tensor_copy', out=wACC[hh][:, 0:nt], in_=wSP[w3][:, 0:nt],
                                     reads=['wSP%d' % w3], writes=['wACC%d' % hh])
                            else:
                                P.op('dve', 'tensor_tensor', out=wACC[hh][:, 0:nt], in0=wACC[hh][:, 0:nt],
                                     in1=wSP[w3][:, 0:nt], op=ALU.add,
                                     reads=['wSP%d' % w3, 'wACC%d' % hh], writes=['wACC%d' % hh])
                            P.op('dve', 'tensor_copy', out=wACCb[hh][:, 0:nt], in_=wACC[hh][:, 0:nt],
                                 reads=['wACC%d' % hh], writes=['wACCb%d' % hh])
                        P.op('act', 'activation', out=wG[w][:, 0:nt], in_=ps[pg][:, 0:nt], func=AF.Exp,
                             reads=['ps%d' % pg], writes=['wG%d' % w])
                        P.op('dve', 'tensor_tensor', out=wA[w3][:, 0:nt], in0=wE[w3][:, 0:nt],
                             in1=wG[w][:, 0:nt], op=ALU.mult,
                             reads=['wE%d' % w3, 'wG%d' % w], writes=['wA%d' % w3])

                    def stageAV(idx):
                        hh, kb = tiles[idx]
                        w = idx % 3
                        first = (kb == kbs[0])
                        last = (kb == kbs[-1])
                        po = 4 + hh
                        P.op('pe', 'matmul', ps[po][:, 0:nt], lhsT=Vv[:, kb, :], rhs=wA[w][:, 0:nt], start=first, stop=last,
                             reads=[kvn, 'wA%d' % w], writes=['ps%d' % po])

                    stageA1(0)
                    if nTl > 1:
                        stageA1(1)
                    stageA2(0)
                    for idx in range(nTl):
                        if idx + 2 < nTl:
                            stageA1(idx + 2)
                        if idx + 1 < nTl:
                            stageA2(idx + 1)
                        stageB(idx)
                    if nTl >= 2:
                        stageAV(nTl - 2)
                    stageAV(nTl - 1)
                    for hh in range(2):
                        pb = hh * 64
                        if hh == 0:
                            P.op('act', 'activation', out=hT[pb:pb + 64, 8 + c, 0:nt], in_=ps[4 + hh][pb:pb + 64, 0:nt],
                                 func=AF.Copy, reads=['ps%d' % (4 + hh)], writes=['hT'])
                        else:
                            P.op('dve', 'tensor_copy', out=hT[pb:pb + 64, 8 + c, 0:nt], in_=ps[4 + hh][pb:pb + 64, 0:nt],
                                 reads=['ps%d' % (4 + hh)], writes=['hT'])
                for grp in range(2 if 'n' in P2PARTS else 0):
                    norm_stats(hT[:, grp * 8:(grp + 1) * 8, :], 8, nt, ps[6], 'ps6', 'hT')
                    rstd_from_ps(ps[6], nt, 1.0 / 1024, wRS, 'wG0', 'ps6')
                    for k in range(8):
                        kc = grp * 8 + k
                        si = next_stg()
                        P.op('dve', 'scalar_tensor_tensor', out=stg[si][:, 0:nt], in0=hT[:, kc, 0:nt], scalar=gcol(l, 32 + kc),
                             in1=wRS[:, 0:nt], op0=ALU.mult, op1=ALU.mult, reads=['hT', 'wG0', 'pvec'], writes=['stg%d' % si])
                        P.dma('sp', 'stg%d' % si, out=MIXT[kc, :, t0:t0 + nt], in_=stg[si][:, 0:nt], reads=['stg%d' % si])

        def phase3(l):
            for (b0, ncb) in groups:
                t0 = b0 * 128
                nt = ncb * 128
                P.dma('sp', 'hT', out=hT[:, :, 0:nt], in_=HT[:, :, t0:t0 + nt].rearrange("k p t -> p k t"), writes=['hT'])
                P.dma('sp', 'xT', out=xT[:, :, 0:nt], in_=MIXT[:, :, t0:t0 + nt].rearrange("k p t -> p k t"), writes=['xT'])
                for s in range(4):
                    wv, wname = load_slab(WB_o[l, :, s * 512:(s + 1) * 512], KC, 512)
                    for oc in range(4):
                        pi = next_ps4()
                        for kc in range(KC):
                            P.op('pe', 'matmul', ps[pi][:, 0:nt], lhsT=wv[:, kc, oc * 128:(oc + 1) * 128], rhs=xT[:, kc, 0:nt],
                                 start=(kc == 0), stop=(kc == KC - 1), reads=[wname, 'xT'], writes=['ps%d' % pi])
                        dc = s * 4 + oc
                        P.op('dve', 'tensor_tensor', out=hT[:, dc, 0:nt], in0=hT[:, dc, 0:nt], in1=ps[pi][:, 0:nt], op=ALU.add,
                             reads=['ps%d' % pi, 'hT'], writes=['hT'])
                big_rmsnorm(l, 16, nt)
                for s in range(11):
                    gv, gname = load_slab(WB_g[l, :, s * 512:(s + 1) * 512], KC, 512)
                    uv, uname = load_slab(WB_u[l, :, s * 512:(s + 1) * 512], KC, 512)
                    for oc in range(4):
                        pg = next_ps4()
                        for kc in range(KC):
                            P.op('pe', 'matmul', ps[pg][:, 0:nt], lhsT=gv[:, kc, oc * 128:(oc + 1) * 128], rhs=xT[:, kc, 0:nt],
                                 start=(kc == 0), stop=(kc == KC - 1), reads=[gname, 'xT'], writes=['ps%d' % pg])
                        pu = next_ps4()
                        for kc in range(KC):
                            P.op('pe', 'matmul', ps[pu][:, 0:nt], lhsT=uv[:, kc, oc * 128:(oc + 1) * 128], rhs=xT[:, kc, 0:nt],
                                 start=(kc == 0), stop=(kc == KC - 1), reads=[uname, 'xT'], writes=['ps%d' % pu])
                        fc = s * 4 + oc
                        gi = fc % 2
                        P.op('act', 'activation', out=wSG[gi][:, 0:nt], in_=ps[pg][:, 0:nt], func=AF.Silu,
                             reads=['ps%d' % pg], writes=[nSG_ % gi])
                        P.op('dve', 'tensor_tensor', out=actT[:, fc, 0:nt], in0=wSG[gi][:, 0:nt], in1=ps[pu][:, 0:nt],
                             op=ALU.mult, reads=[nSG_ % gi, 'ps%d' % pu], writes=['actT'])
                for dcg in range(4):
                    for q in range(4):
                        dv, dname = load_slab(WB_d[l, q * 11 * 128:(q + 1) * 11 * 128, dcg * 512:(dcg + 1) * 512], 11, 512)
                        for dcl in range(4):
                            for j in range(11):
                                P.op('pe', 'matmul', ps[4 + dcl][:, 0:nt], lhsT=dv[:, j, dcl * 128:(dcl + 1) * 128],
                                     rhs=actT[:, q * 11 + j, 0:nt], start=(q == 0 and j == 0), stop=(q == 3 and j == 10),
                                     reads=[dname, 'actT'], writes=['ps%d' % (4 + dcl)])
                    for dcl in range(4):
                        dc = dcg * 4 + dcl
                        P.op('dve', 'tensor_tensor', out=hT[:, dc, 0:nt], in0=hT[:, dc, 0:nt], in1=ps[4 + dcl][:, 0:nt],
                             op=ALU.add, reads=['ps%d' % (4 + dcl), 'hT'], writes=['hT'])
                P.dma('sp', 'hst', out=HT[:, :, t0:t0 + nt].rearrange("k p t -> p k t"), in_=hT[:, :, 0:nt], reads=['hT'])

        convert_weights(0)
        phase0()
        P.barrier()
        for l in range(DEPTH):
            wl[0] = l
            if phases is None or 1 in phases:
                phase1(l)
                P.barrier()
            if l + 1 < DEPTH:
                convert_weights(l + 1)
            if phases is None or 2 in phases:
                phase2(l)
                P.barrier()
            if phases is None or 3 in phases:
                phase3(l)
                P.barrier(new_epoch=(l % 2 == 1 and l != DEPTH - 1))
        phaseF()
        P.barrier()
        P.emit()
        nops = {e: len(P.ops[e]) for e in ENGS}
        print("plan ops:", nops, "sems:", len(P.sems))
    return nc


def host_consts(NB):
    h = np.arange(1, 17, dtype=np.float32)
    slopes = np.exp2(-8.0 * h / 16).astype(np.float32)
    s = np.arange(128)[:, None].astype(np.float32)
    t = np.arange(128)[None, :].astype(np.float32)
    bias4 = np.full((128, 4, 16, 128), NEGB, np.float32)
    for hh in range(16):
        sl = slopes[hh]
        d = t - s
        bias4[:, 0, hh, :] = np.where(d >= 0, -sl * d, NEGB)
        d1 = t - s + 128
        bias4[:, 1, hh, :] = np.where(d1 < 128, -sl * d1, NEGB)
        bias4[:, 2, hh, :] = np.where(s >= 112, -sl * d, NEGB)
        bias4[:, 3, hh, :] = np.where((s >= 112) & (d >= 0), -sl * d, NEGB)
    ctab = np.zeros((128, NB, 16), np.float32)
    for b in range(NB):
        ctab[:, b, :] = -slopes[None, :] * 128.0 * b
    sbm = np.zeros((128, 6, 4, 128), np.float32)
    for m in range(4):
        for c in range(4):
            if c > m:
                sbm[:, m, c, :] = 1.0
            elif c == m:
                sbm[:, m, c, :] = (s < t)
    sbm[112:, 4, :, :] = 1.0
    sbm[:, 5, 0, :] = ((s >= 112) & (s < t))
    j = np.arange(128)[:, None]
    si = np.arange(128)[None, :]
    cbf = np.zeros((128, 3, 128), np.float32)
    cbf[:, 0, :] = -(j >= si).astype(np.float32)
    cbf[:, 1, :] = -1.0
    cbf[:, 2, :] = 1.0
    cf32 = np.zeros((128, 3, 128), np.float32)
    cf32[:, 0, :] = 1.0
    cf32[:, 1, :] = ((j // 64) == (si // 64)).astype(np.float32)
    cf32[:, 2, :] = np.eye(128, dtype=np.float32)
    return dict(bias4=bias4.reshape(128, -1), ctab=ctab.reshape(128, -1), sbm=sbm.reshape(128, -1),
                cbf=cbf.reshape(128, -1), cf32=cf32.reshape(128, -1))


def make_inputs(NB, DEPTH, x, meta_tokens, attn_norm_g, w_in, q_norm_g, k_norm_g, attn_sinks,
                swa_out_g, sb_out_g, w_o, ffn_norm_g, w_gate, w_up, w_down):
    consts = host_consts(NB)
    f = lambda a: np.ascontiguousarray(np.asarray(a, dtype=np.float32))
    B = x.shape[0]
    pvec = np.zeros((128, DEPTH * NPV), np.float32)
    for l in range(DEPTH):
        o = l * NPV
        pvec[:, o:o + 16] = f(attn_norm_g[l]).reshape(16, 128).T
        pvec[:, o + 16:o + 32] = f(ffn_norm_g[l]).reshape(16, 128).T
        pvec[:, o + 32:o + 40] = f(swa_out_g[l]).reshape(8, 128).T
        pvec[:, o + 40:o + 48] = f(sb_out_g[l]).reshape(8, 128).T
        pvec[:, o + 48] = np.tile(f(q_norm_g[l]), 2)
        pvec[:, o + 49] = np.tile(f(k_norm_g[l]), 2)
    sinks = f(attn_sinks).reshape(1, DEPTH * 16)
    shared = dict(w_in=f(w_in), w_o=f(w_o), w_gate=f(w_gate), w_up=f(w_up), w_down=f(w_down),
                  pvec=pvec, sinks=sinks, **consts)
    in_maps = []
    xs = {}
    for core in range(8):
        b = core % B
        if b not in xs:
            xi = np.zeros((NB * 128, D), np.float32)
            xi[112:128] = f(meta_tokens)
            xi[128:] = f(x[b])
            xs[b] = xi
        m = dict(shared)
        m["xin"] = xs[b]
        in_maps.append(m)
    return in_maps


_NC_CACHE = {}
PHASES = None
DEBUG = False
import os as _os
P2PARTS = _os.environ.get('P2PARTS', 'abn12345')


def kernel(x, meta_tokens, attn_norm_g, w_in, q_norm_g, k_norm_g, attn_sinks,
           swa_out_g, sb_out_g, w_o, ffn_norm_g, w_gate, w_up, w_down):
    x = np.asarray(x)
    B, S, _ = x.shape
    NB = S // 128 + 1
    DEPTH = np.asarray(w_in).shape[0]
    key = (NB, DEPTH)
    if key not in _NC_CACHE:
        _NC_CACHE[key] = build(NB, DEPTH, debug=DEBUG, phases=PHASES)
    nc = _NC_CACHE[key]
    in_maps = make_inputs(NB, DEPTH, x, meta_tokens, attn_norm_g, w_in, q_norm_g, k_norm_g, attn_sinks,
                          swa_out_g, sb_out_g, w_o, ffn_norm_g, w_gate, w_up, w_down)
    res = run_bass_kernel_spmd(nc, in_maps, core_ids=list(range(8)))
    out = np.stack([np.asarray(res.results[b]["out"], dtype=np.float32) for b in range(B)], axis=0)
    if DEBUG:
        global LAST_RES
        LAST_RES = res.results
    return out
```

```python
import numpy as np
from contextlib import ExitStack
import concourse.bass as bass
import concourse.mybir as mybir
from concourse.bass_utils import run_bass_kernel_spmd

F32 = mybir.dt.float32
BF16 = mybir.dt.bfloat16
AF = mybir.ActivationFunctionType
ALU = mybir.AluOpType

D = 2048
KC = 16
PW = 4608
DFF = 5632
FC = 44
EPS = 1e-6
NEGB = -30000.0
NPV = 50
ENGS = ['pe', 'act', 'dve', 'pool', 'sp']
SAME_ENGINE_SYNC = True


class Plan:
    def __init__(self, nc, stack):
        self.nc = nc
        self.stack = stack
        self.ops = {e: [] for e in ENGS}
        self.last_w = {}
        self.readers = {}
        self.known = {e: {} for e in ENGS}
        self.epoch = 0
        self.sems = {}
        self.dma_cnt = {}

    def _sem(self, key):
        k = (key, self.epoch)
        if k not in self.sems:
            self.sems[k] = self.stack.enter_context(
                self.nc.semaphore("s%d_%s" % (self.epoch, key.replace(':', '_'))))
        return k

    def _deps(self, eng, reads, writes):
        deps = []
        for r in reads:
            d = self.last_w.get(r)
            if d is not None:
                deps.append(d)
        for w in writes:
            d = self.last_w.get(w)
            if d is not None:
                deps.append(d)
            deps.extend(self.readers.get(w, ()))
        waits = {}
        for (key, n) in deps:
            if key == eng and (eng == 'pe' or not SAME_ENGINE_SYNC):
                continue
            if self.known[eng].get(key, 0) >= n:
                continue
            if waits.get(key, 0) < n:
                waits[key] = n
        for key, n in waits.items():
            self.known[eng][key] = n
        return waits

    def _mark(self, me, reads, writes):
        for r in reads:
            self.readers.setdefault(r, []).append(me)
        for w in writes:
            self.last_w[w] = me
            self.readers[w] = []

    def op(self, eng, meth, *args, reads=(), writes=(), **kw):
        fn = (meth, args, kw)
        waits = self._deps(eng, reads, writes)
        semk = self._sem(eng)
        idx = len(self.ops[eng])
        rec = dict(kind='op', fn=fn, waits=[(self._sem(k), k, n) for k, n in waits.items()],
                   sem=semk, sig=False, idx=idx)
        self.ops[eng].append(rec)
        self._mark((eng, idx + 1), reads, writes)

    def dma(self, eng, semname, reads=(), writes=(), **kw):
        fn = ('dma_start', (), kw)
        key = 'dma:' + semname
        n_prev = self.dma_cnt.get((key, self.epoch), 0)
        waits = self._deps(eng, reads, writes)
        if n_prev > 0 and self.known[eng].get(key, 0) < n_prev:
            waits[key] = max(waits.get(key, 0), n_prev)
            self.known[eng][key] = n_prev
        semk = self._sem(key)
        n = n_prev + 1
        self.dma_cnt[(key, self.epoch)] = n
        rec = dict(kind='dma', fn=fn, waits=[(self._sem(k), k, m) for k, m in waits.items()],
                   sem=semk, idx=len(self.ops[eng]))
        self.ops[eng].append(rec)
        self._mark((key, n), reads, writes)

    def barrier(self, new_epoch=False):
        finals = []
        for e in ENGS:
            for rec in reversed(self.ops[e]):
                if rec['kind'] == 'op':
                    if rec['sem'][1] == self.epoch:
                        finals.append((e, rec['idx'] + 1))
                    break
        for (key, ep), n in self.dma_cnt.items():
            if ep == self.epoch:
                finals.append((key, n))
        for e in ENGS:
            waits = [(self._sem(key), key, n) for key, n in finals if key != e]
            self.ops[e].append(dict(kind='wait', waits=waits, idx=len(self.ops[e])))
        self.last_w = {}
        self.readers = {}
        if new_epoch:
            self.epoch += 1
            self.known = {e: {} for e in ENGS}
        else:
            for e in ENGS:
                for key, n in finals:
                    if key != e:
                        self.known[e][key] = max(self.known[e].get(key, 0), n)

    def emit(self):
        nc = self.nc
        for e in ENGS:
            for rec in self.ops[e]:
                for (semk, key, n) in rec['waits']:
                    if not key.startswith('dma:'):
                        self.ops[key][n - 1]['sig'] = True
        val = {}
        for e in ENGS:
            c = {}
            for rec in self.ops[e]:
                if rec['kind'] == 'op' and rec['sig']:
                    ep = rec['sem'][1]
                    c[ep] = c.get(ep, 0) + 1
                    val[(e, rec['idx'] + 1)] = c[ep]
        engobj = {'pe': 'tensor', 'act': 'scalar', 'dve': 'vector', 'pool': 'gpsimd', 'sp': 'sync'}
        with nc.Block() as block:
            for e in ENGS:
                def body(eng, e=e):
                    for rec in self.ops[e]:
                        for (semk, key, n) in rec['waits']:
                            v = 16 * n if key.startswith('dma:') else val[(key, n)]
                            eng.wait_ge(self.sems[semk], v)
                        if rec['kind'] == 'op':
                            m, a, k = rec['fn']
                            ins = getattr(eng, m)(*a, **k)
                            if rec['sig']:
                                ins.then_inc(self.sems[rec['sem']], 1)
                        elif rec['kind'] == 'dma':
                            m, a, k = rec['fn']
                            getattr(eng, m)(*a, **k).then_inc(self.sems[rec['sem']], 16)
                getattr(block, engobj[e])(body)


def groups_of(NB):
    gs = [(0, 1)]
    b = 1
    while b < NB:
        n = min(4, NB - b)
        gs.append((b, n))
        b += n
    return gs


def build(NB, DEPTH, debug=False, phases=None):
    LP = NB * 128
    nc = bass.Bass("TRN2", target_bir_lowering=False)
    dt_in = lambda name, shape: nc.dram_tensor(name, shape, F32, kind="ExternalInput").ap()
    xin = dt_in("xin", [LP, D])
    w_in = dt_in("w_in", [DEPTH, D, PW])
    w_o = dt_in("w_o", [DEPTH, D, D])
    w_gate = dt_in("w_gate", [DEPTH, D, DFF])
    w_up = dt_in("w_up", [DEPTH, D, DFF])
    w_down = dt_in("w_down", [DEPTH, DFF, D])
    pvec_d = dt_in("pvec", [128, DEPTH * NPV])
    sinks_d = dt_in("sinks", [1, DEPTH * 16])
    bias4_d = dt_in("bias4", [128, 4 * 16 * 128])
    ctab_d = dt_in("ctab", [128, NB * 16])
    sbm_d = dt_in("sbm", [128, 6 * 512])
    cbf_d = dt_in("cbf", [128, 3 * 128])
    cf32_d = dt_in("cf32", [128, 3 * 128])
    out_d = nc.dram_tensor("out", [LP - 128, D], F32, kind="ExternalOutput").ap()
    skind = "ExternalOutput" if debug else "Internal"
    scr = lambda name, shape, dt: nc.dram_tensor(name, shape, dt, kind=skind).ap()
    HT = scr("HT", [KC, 128, LP], F32)
    QTs = scr("QTs", [8, 128, LP], BF16)
    KTs = scr("KTs", [8, 128, LP], BF16)
    Vs = scr("Vs", [LP, 1024], BF16)
    QTa = scr("QTa", [8, 128, LP], BF16)
    KTa = scr("KTa", [4, 64, LP], BF16)
    Va = scr("Va", [LP, 256], BF16)
    MIXT = scr("MIXT", [KC, 128, LP], BF16)
    WB_in = scr("WB_in", [DEPTH, D, PW], BF16)
    WB_o = scr("WB_o", [DEPTH, D, D], BF16)
    WB_g = scr("WB_g", [DEPTH, D, DFF], BF16)
    WB_u = scr("WB_u", [DEPTH, D, DFF], BF16)
    WB_d = scr("WB_d", [DEPTH, DFF, D], BF16)

    groups = groups_of(NB)

    with ExitStack() as st:
        sb = lambda name, shape, dt: st.enter_context(nc.sbuf_tensor("s_" + name, shape, dt))
        B_h = sb("B_h", [128, KC * 512], F32)
        B_x = sb("B_x", [128, KC * 512], BF16)
        NSL = 3
        SLW = max(8704, 2 * LP)
        slabs = [sb("slab%d" % i, [128, SLW], BF16) for i in range(NSL)]
        B_act = sb("B_act", [128, FC * 512], BF16)
        pvec = sb("pvec", [128, DEPTH * NPV], F32)
        esink = sb("esink", [128, DEPTH * 16], F32)
        ctab = sb("ctab", [128, NB * 16], F32)
        sbm = sb("sbm", [128, 6 * 512], BF16)
        cbf = sb("cbf", [128, 3 * 128], BF16)
        cf32 = sb("cf32", [128, 3 * 128], F32)
        wE = [sb("wE%d" % i, [128, 512], F32) for i in range(4)]
        wSP = [sb("wSP%d" % i, [128, 512], BF16) for i in range(3)]
        wG = [sb("wG%d" % i, [128, 512], F32) for i in range(2)]
        wA = [sb("wA%d" % i, [128, 512], BF16) for i in range(3)]
        wACC = [sb("wACC%d" % i, [128, 512], F32) for i in range(2)]
        wACCb = [sb("wACCb%d" % i, [128, 512], BF16) for i in range(2)]
        stg = [sb("stg%d" % i, [128, 512], BF16) for i in range(3)]
        wRS = wG[0]
        wT, nT_ = wG, 'wG%d'
        wR, nR_ = wE, 'wE%d'
        wSQ, nSQ_ = wACC, 'wACC%d'
        wRS2, nRS2_ = wG, 'wG%d'
        wSG, nSG_ = wE, 'wE%d'
        wPT = [wSP[0], wSP[1], wA[0]]
        nPT = ['wSP0', 'wSP1', 'wA0']
        ps = [st.enter_context(nc.psum_tensor("ps%d" % i, [128, 512], F32)) for i in range(8)]

        P = Plan(nc, st)
        TRI = cbf[:, 0:128]
        ONESNEG = cbf[:, 128:256]
        ONESB = cbf[:, 256:384]
        ONESF = cf32[:, 0:128]
        BDF = cf32[:, 128:256]
        IDF = cf32[:, 256:384]

        hT = B_h[:, :].rearrange("p (k t) -> p k t", k=KC)
        xT = B_x[:, :].rearrange("p (k t) -> p k t", k=KC)
        actT = B_act[:, :].rearrange("p (k t) -> p k t", k=FC)
        bias4 = B_act[:, 0:16384].bitcast(F32).rearrange("p (s h t) -> p s h t", s=4, h=16)
        KA = B_act[:, 16384:16384 + 3072].rearrange("p (j t) -> p j t", j=4)
        VA = B_act[:, 19456:19456 + 3072].rearrange("p (b j d) -> p b j d", b=6, j=4)
        Qs = B_x[:, 0:4096].rearrange("p (c t) -> p c t", c=8)
        Qa16 = slabs[2][:, 0:8192].rearrange("p (h t) -> p h t", h=16)
        tokbuf = B_act[:, 0:4096].bitcast(F32)

        stg_i = [0]

        def next_stg():
            i = stg_i[0] % 3
            stg_i[0] += 1
            return i

        slab_i = [0]

        def load_slab(src_ap, nk, ncols):
            i = slab_i[0] % NSL
            slab_i[0] += 1
            view = slabs[i][:, 0:nk * ncols].rearrange("p (k n) -> p k n", k=nk)
            src = src_ap.rearrange("(k p) n -> p k n", p=128)
            P.dma('pool', 'slab%d' % i, out=view, in_=src, reads=['WB%d' % wl[0]], writes=['slab%d' % i])
            return view, 'slab%d' % i

        wl = [0]

        cv_i = [0]

        def convert_weights(l):
            for (src, dst, rows) in ((w_in, WB_in, D), (w_o, WB_o, D), (w_gate, WB_g, D), (w_up, WB_u, D), (w_down, WB_d, DFF)):
                for r0 in range(0, rows, 512):
                    r1 = min(rows, r0 + 512)
                    i = cv_i[0] % 4
                    cv_i[0] += 1
                    P.dma('pool', 'cv%d' % i, out=dst[l, r0:r1, :], in_=src[l, r0:r1, :], writes=['WB%d' % l])

        P.dma('sp', 'c0', out=pvec[:, :], in_=pvec_d[:, :], writes=['pvec'])
        P.dma('sp', 'c1', out=ctab[:, :], in_=ctab_d[:, :], writes=['ctab'])
        P.dma('sp', 'c2', out=cf32[:, :], in_=cf32_d[:, :], writes=['cf32'])
        P.dma('sp', 'c3', out=esink[:, :], in_=sinks_d[:, :].broadcast_to([128, DEPTH * 16]), writes=['esink'])
        P.dma('pool', 'c4', out=sbm[:, :], in_=sbm_d[:, :], writes=['sbm'])
        P.dma('pool', 'c5', out=cbf[:, :], in_=cbf_d[:, :], writes=['cbf'])
        P.op('act', 'activation', out=esink[:, :], in_=esink[:, :], func=AF.Exp, reads=['esink'], writes=['esink'])

        qsc = sb("qsc", [128, DEPTH], F32)
        for l_ in range(DEPTH):
            P.op('act', 'activation', out=qsc[:, l_:l_ + 1], in_=pvec[:, l_ * NPV + 48:l_ * NPV + 49], func=AF.Copy,
                 scale=0.125, reads=['pvec'], writes=['qsc'])

        def gcol(l, j):
            return pvec[:, l * NPV + j: l * NPV + j + 1]

        def rstd_from_ps(psb, nt, inv_n, out_tile, wname, psname):
            P.op('act', 'activation', out=out_tile[:, 0:nt], in_=psb[:, 0:nt], func=AF.Sqrt, scale=inv_n, bias=EPS,
                 reads=[psname], writes=[wname])
            P.op('dve', 'reciprocal', out=out_tile[:, 0:nt], in_=out_tile[:, 0:nt], reads=[wname], writes=[wname])

        sq_i = [0]

        def norm_stats(src_view, nk, nt, psb, psname, src_name):
            for k in range(nk):
                i = sq_i[0] % 2
                sq_i[0] += 1
                P.op('act', 'activation', out=wSQ[i][:, 0:nt], in_=src_view[:, k, 0:nt], func=AF.Square,
                     reads=[src_name], writes=[nSQ_ % i])
                P.op('pe', 'matmul', psb[:, 0:nt], lhsT=ONESF, rhs=wSQ[i][:, 0:nt], start=(k == 0), stop=(k == nk - 1),
                     reads=[nSQ_ % i, 'cf32'], writes=[psname])

        def big_rmsnorm(l, gbase, nt):
            norm_stats(hT, KC, nt, ps[6], 'ps6', 'hT')
            rstd_from_ps(ps[6], nt, 1.0 / D, wRS, 'wG0', 'ps6')
            for kc in range(KC):
                P.op('dve', 'scalar_tensor_tensor', out=xT[:, kc, 0:nt], in0=hT[:, kc, 0:nt], scalar=gcol(l, gbase + kc),
                     in1=wRS[:, 0:nt], op0=ALU.mult, op1=ALU.mult, reads=['hT', 'wG0', 'pvec'], writes=['xT'])

        def store_chunk(dst_ap, src_psb, psname, nt):
            i = next_stg()
            P.op('act', 'activation', out=stg[i][:, 0:nt], in_=src_psb[:, 0:nt], func=AF.Copy,
                 reads=[psname], writes=['stg%d' % i])
            P.dma('sp', 'stg%d' % i, out=dst_ap, in_=stg[i][:, 0:nt], reads=['stg%d' % i])

        psr = [0]

        def next_ps4():
            i = psr[0] % 4
            psr[0] += 1
            return i

        def phase0():
            for b in range(NB):
                P.dma('sp', 'tok', out=tokbuf, in_=xin[b * 128:(b + 1) * 128, :], writes=['tokbuf'])
                for q in range(4):
                    pi = next_ps4()
                    for j in range(4):
                        kc = q * 4 + j
                        P.op('pe', 'transpose', ps[pi][:, j * 128:(j + 1) * 128], tokbuf[:, kc * 128:(kc + 1) * 128], IDF,
                             reads=['tokbuf', 'cf32'], writes=['ps%d' % pi])
                    src = ps[pi][:, :].rearrange("p (a t) -> p a t", a=4)
                    dst = hT[:, q * 4:(q + 1) * 4, 0:128]
                    if q % 2 == 0:
                        P.op('act', 'activation', out=dst, in_=src, func=AF.Copy, reads=['ps%d' % pi], writes=['hT'])
                    else:
                        P.op('dve', 'tensor_copy', out=dst, in_=src, reads=['ps%d' % pi], writes=['hT'])
                P.dma('sp', 'hst', out=HT[:, :, b * 128:(b + 1) * 128].rearrange("k p t -> p k t"), in_=hT[:, :, 0:128],
                      reads=['hT'])

        def phaseF():
            tb4 = tokbuf.rearrange("p (q n) -> p q n", q=4)
            for b in range(1, NB):
                P.dma('sp', 'hT', out=hT[:, :, 0:128], in_=HT[:, :, b * 128:(b + 1) * 128].rearrange("k p t -> p k t"),
                      writes=['hT'])
                for q in range(4):
                    pi = next_ps4()
                    for j in range(4):
                        kc = q * 4 + j
                        P.op('pe', 'transpose', ps[pi][:, j * 128:(j + 1) * 128], hT[:, kc, 0:128], IDF,
                             reads=['hT', 'cf32'], writes=['ps%d' % pi])
                    if q % 2 == 0:
                        P.op('act', 'activation', out=tb4[:, q, :], in_=ps[pi][:, :], func=AF.Copy,
                             reads=['ps%d' % pi], writes=['tokbuf'])
                    else:
                        P.op('dve', 'tensor_copy', out=tb4[:, q, :], in_=ps[pi][:, :], reads=['ps%d' % pi], writes=['tokbuf'])
                P.dma('sp', 'tok', out=out_d[(b - 1) * 128:b * 128, :], in_=tokbuf, reads=['tokbuf'])

        def phase1(l):
            for (b0, ncb) in groups:
                t0 = b0 * 128
                nt = ncb * 128
                P.dma('sp', 'hT', out=hT[:, :, 0:nt], in_=HT[:, :, t0:t0 + nt].rearrange("k p t -> p k t"), writes=['hT'])
                big_rmsnorm(l, 0, nt)
                for s in range(9):
                    wv, wname = load_slab(WB_in[l, :, s * 512:(s + 1) * 512], KC, 512)
                    if s <= 6:
                        noc = 4 if s != 2 else 2
                        for oc in range(noc):
                            pi = next_ps4()
                            pn = 'ps%d' % pi
                            for kc in range(KC):
                                P.op('pe', 'matmul', ps[pi][:, 0:nt], lhsT=wv[:, kc, oc * 128:(oc + 1) * 128],
                                     rhs=xT[:, kc, 0:nt], start=(kc == 0), stop=(kc == KC - 1),
                                     reads=[wname, 'xT'], writes=[pn])
                            if s in (3, 4):
                                store_chunk(QTs[(s - 3) * 4 + oc, :, t0:t0 + nt], ps[pi], pn, nt)
                            elif s in (5, 6):
                                store_chunk(KTs[(s - 5) * 4 + oc, :, t0:t0 + nt], ps[pi], pn, nt)
                            else:
                                i = sq_i[0] % 2
                                sq_i[0] += 1
                                pj = 4 + i
                                pjn = 'ps%d' % pj
                                P.op('act', 'activation', out=wSQ[i][:, 0:nt], in_=ps[pi][:, 0:nt], func=AF.Square,
                                     reads=[pn], writes=[nSQ_ % i])
                                P.op('pe', 'matmul', ps[pj][:, 0:nt], lhsT=BDF, rhs=wSQ[i][:, 0:nt], start=True, stop=True,
                                     reads=[nSQ_ % i, 'cf32'], writes=[pjn])
                                rstd_from_ps(ps[pj], nt, 1.0 / 64, wRS2[i], nRS2_ % i, pjn)
                                si = next_stg()
                                gc = qsc[:, l:l + 1] if s < 2 else gcol(l, 49)
                                P.op('dve', 'scalar_tensor_tensor', out=stg[si][:, 0:nt], in0=ps[pi][:, 0:nt], scalar=gc,
                                     in1=wRS2[i][:, 0:nt], op0=ALU.mult, op1=ALU.mult,
                                     reads=[pn, nRS2_ % i, 'pvec', 'qsc'], writes=['stg%d' % si])
                                if s < 2:
                                    P.dma('sp', 'stg%d' % si, out=QTa[s * 4 + oc, :, t0:t0 + nt], in_=stg[si][:, 0:nt],
                                          reads=['stg%d' % si])
                                else:
                                    P.dma('sp', 'stg%d' % si,
                                          out=KTa[2 * oc:2 * oc + 2, :, t0:t0 + nt].rearrange("j p t -> (j p) t"),
                                          in_=stg[si][:, 0:nt], reads=['stg%d' % si])
                    if s in (2, 7, 8):
                        c0, ncol = (256, 256) if s == 2 else (0, 512)
                        for tb in range(ncb):
                            pi = next_ps4()
                            pn = 'ps%d' % pi
                            for kc in range(KC):
                                P.op('pe', 'matmul', ps[pi][:, 0:ncol], lhsT=xT[:, kc, tb * 128:(tb + 1) * 128],
                                     rhs=wv[:, kc, c0:c0 + ncol], start=(kc == 0), stop=(kc == KC - 1),
                                     reads=[wname, 'xT'], writes=[pn])
                            r0 = t0 + tb * 128
                            if s == 2:
                                store_chunk(Va[r0:r0 + 128, :], ps[pi], pn, 256)
                            else:
                                store_chunk(Vs[r0:r0 + 128, (s - 7) * 512:(s - 6) * 512], ps[pi], pn, 512)

        def phase2(l):
            P.dma('sp', 'bias', out=B_act[:, 0:16384].bitcast(F32), in_=bias4_d[:, :], writes=['bias4'])
            kv_i = [0]
            tile_i = [0]
            for (b0, ncb) in groups:
                t0 = b0 * 128
                nt = ncb * 128
                P.dma('sp', 'qs', out=Qs[:, :, 0:nt], in_=QTs[:, :, t0:t0 + nt].rearrange("c p t -> p c t"), writes=['Qs'])
                P.dma('sp', 'qa', out=Qa16[0:64, :, 0:nt],
                      in_=QTa.rearrange("c (two p) t -> (c two) p t", two=2)[:, :, t0:t0 + nt].rearrange("h p t -> p h t"),
                      writes=['slab2'])
                kb_lo = max(b0 - 1, 0)
                nkb = b0 + ncb - kb_lo
                P.dma('sp', 'ka0', out=KA[0:64, :, 0:128], in_=KTa[:, :, 0:128].rearrange("j p t -> p j t"),
                      writes=['KA'])
                P.dma('sp', 'ka0', out=KA[0:64, :, 128:128 + nkb * 128],
                      in_=KTa[:, :, kb_lo * 128:(kb_lo + nkb) * 128].rearrange("j p t -> p j t"), writes=['KA'])
                for half in range(2):
                    hs = slice(half * 64, (half + 1) * 64)
                    P.dma('sp', 'va%d' % half, out=VA[:, 0, :, hs], in_=Va[0:128, :].rearrange("p (j d) -> p j d", j=4),
                          writes=['VA'])
                    for kk in range(nkb):
                        P.dma('sp', 'va%d' % half, out=VA[:, 1 + kk, :, hs],
                              in_=Va[(kb_lo + kk) * 128:(kb_lo + kk + 1) * 128, :].rearrange("p (j d) -> p j d", j=4),
                              writes=['VA'])
                for bi in range(ncb if 'a' in P2PARTS else 0):
                    b = b0 + bi
                    cb = bi * 128
                    if b == 0:
                        segs = [(3, 0, 0)]
                    else:
                        segs = []
                        if b >= 2:
                            segs.append((1, 128 + (b - 1 - kb_lo) * 128, 1 + (b - 1 - kb_lo)))
                        segs.append((0, 128 + (b - kb_lo) * 128, 1 + (b - kb_lo)))
                        segs.append((2, 0, 0))
                    ns = len(segs)
                    for j in range(4):
                        for si, (bidx, kcol, vblk) in enumerate(segs):
                            pi = si
                            pn = 'ps%d' % pi
                            for u in range(4):
                                h = 4 * j + u
                                hb = (u % 2) * 64
                                P.op('pe', 'matmul', ps[pi][:, u * 128:(u + 1) * 128], lhsT=KA[0:64, j, kcol:kcol + 128],
                                     rhs=Qa16[0:64, h, cb:cb + 128], start=True, stop=True,
                                     reads=['KA', 'slab2'], writes=[pn])
                            ti = tile_i[0] % 2
                            tile_i[0] += 1
                            tv = wT[ti][:, :].rearrange("p (u t) -> p u t", u=4)
                            if '4' not in P2PARTS:
                                continue
                            P.op('dve', 'tensor_tensor', out=tv, in0=ps[pi][:, :].rearrange("p (u t) -> p u t", u=4),
                                 in1=bias4[:, bidx, 4 * j:4 * j + 4, :], op=ALU.add,
                                 reads=[pn, 'bias4'], writes=[nT_ % ti])
                            if bidx == 2 and '1' in P2PARTS:
                                P.op('dve', 'tensor_tensor', out=tv, in0=tv,
                                     in1=ctab[:, b * 16 + 4 * j:b * 16 + 4 * j + 4].unsqueeze(2).broadcast_to([128, 4, 128]),
                                     op=ALU.add, reads=[nT_ % ti, 'ctab'], writes=[nT_ % ti])
                            if '5' in P2PARTS:
                                P.op('act', 'activation', out=wPT[si][:, :], in_=wT[ti][:, :], func=AF.Exp,
                                     reads=[nT_ % ti], writes=[nPT[si]])
                        for u in range(4 if '2' in P2PARTS else 0):
                            for si, (bidx, kcol, vblk) in enumerate(segs):
                                P.op('pe', 'matmul', ps[3][:, u * 128:(u + 1) * 128], lhsT=VA[:, vblk, j, :],
                                     rhs=wPT[si][:, u * 128:(u + 1) * 128], start=(si == 0), stop=(si == ns - 1),
                                     reads=['VA', nPT[si]], writes=['ps3'])
                        for u in range(4 if '2' in P2PARTS else 0):
                            for si, (bidx, kcol, vblk) in enumerate(segs):
                                P.op('pe', 'matmul', ps[4][:, u * 128:(u + 1) * 128], lhsT=ONESB,
                                     rhs=wPT[si][:, u * 128:(u + 1) * 128], start=(si == 0), stop=(si == ns - 1),
                                     reads=['cbf', nPT[si]], writes=['ps4'])
                        if '3' not in P2PARTS:
                            continue
                        ri = tile_i[0] % 2
                        rv = wR[ri][:, :].rearrange("p (u t) -> p u t", u=4)
                        P.op('dve', 'tensor_tensor', out=rv, in0=ps[4][:, :].rearrange("p (u t) -> p u t", u=4),
                             in1=esink[:, l * 16 + 4 * j:l * 16 + 4 * j + 4].unsqueeze(2).broadcast_to([128, 4, 128]),
                             op=ALU.add, reads=['ps4', 'esink'], writes=[nR_ % ri])
                        P.op('dve', 'reciprocal', out=wR[ri][:, :], in_=wR[ri][:, :], reads=[nR_ % ri], writes=[nR_ % ri])
                        for par in range(2):
                            pb = par * 64
                            P.op('dve', 'tensor_tensor', out=hT[pb:pb + 64, 2 * j:2 * j + 2, cb:cb + 128],
                                 in0=ps[3][pb:pb + 64, :].rearrange("p (a u t) -> p a u t", a=2, u=2)[:, :, par, :],
                                 in1=wR[ri][pb:pb + 64, :].rearrange("p (a u t) -> p a u t", a=2, u=2)[:, :, par, :],
                                 op=ALU.mult, reads=['ps3', nR_ % ri], writes=['hT'])
                kmax = b0 + ncb
                kbs = list(range(kmax - 1, -1, -1))
                ncs = 8 if 'b' in P2PARTS else 0
                tiles = [(c, hh, kb) for c in range(ncs) for kb in kbs for hh in range(2)]
                nTl = len(tiles)
                kvv = {}

                def load_kv(c):
                    ki = kv_i[0] % 2
                    kv_i[0] += 1
                    KTv = slabs[ki][:, 0:LP]
                    Vv = slabs[ki][:, LP:2 * LP].rearrange("p (b d) -> p b d", d=128)
                    kvn = 'slab%d' % ki
                    P.dma('sp', 'kt%d' % ki, out=KTv[:, 0:kmax * 128], in_=KTs[c, :, 0:kmax * 128], writes=[kvn])
                    for v0 in range(0, kmax, 8):
                        v1 = min(kmax, v0 + 8)
                        P.dma('sp', 'v%d_%d' % (ki, v0 // 8), out=Vv[:, v0:v1, :],
                              in_=Vs[v0 * 128:v1 * 128, c * 128:(c + 1) * 128].rearrange("(b p) d -> p b d", p=128),
                              writes=[kvn])
                    kvv[c] = (KTv, Vv, kvn)

                def mask_of(kb):
                    if b0 == 0:
                        return 5
                    if kb == 0:
                        return 4
                    if kb >= b0:
                        return kb - b0
                    return None

                def stageA1(idx):
                    c, hh, kb = tiles[idx]
                    KTv, Vv, kvn = kvv[c]
                    hb = hh * 64
                    w4 = idx % 4
                    pi = idx % 2
                    pn = 'ps%d' % pi
                    P.op('pe', 'matmul', ps[pi][:, 0:nt], lhsT=KTv[hb:hb + 64, kb * 128:(kb + 1) * 128],
                         rhs=Qs[hb:hb + 64, c, 0:nt], start=True, stop=True, reads=[kvn, 'Qs'], writes=[pn])
                    P.op('act', 'activation', out=wE[w4][:, 0:nt], in_=ps[pi][:, 0:nt], func=AF.Exp, scale=0.125,
                         reads=[pn], writes=['wE%d' % w4])

                def stageA2(idx):
                    c, hh, kb = tiles[idx]
                    w3 = idx % 3
                    w4 = idx % 4
                    P.op('act', 'activation', out=wSP[w3][:, 0:nt], in_=wE[w4][:, 0:nt], func=AF.Ln, bias=1.0,
                         reads=['wE%d' % w4], writes=['wSP%d' % w3])
                    m = mask_of(kb)
                    if m is not None:
                        P.op('dve', 'tensor_tensor', out=wSP[w3][:, 0:nt], in0=wSP[w3][:, 0:nt],
                             in1=sbm[:, m * 512:m * 512 + nt], op=ALU.mult,
                             reads=['wSP%d' % w3, 'sbm'], writes=['wSP%d' % w3])
                        P.op('dve', 'tensor_tensor', out=wE[w4][:, 0:nt], in0=wE[w4][:, 0:nt],
                             in1=sbm[:, m * 512:m * 512 + nt], op=ALU.mult,
                             reads=['wE%d' % w4, 'sbm'], writes=['wE%d' % w4])

                def stageB(idx):
                    c, hh, kb = tiles[idx]
                    w3 = idx % 3
                    w = idx % 2
                    first = (kb == kbs[0])
                    last = (kb == kbs[-1])
                    pg = 2 + w
                    P.op('pe', 'matmul', ps[pg][:, 0:nt], lhsT=TRI, rhs=wSP[w3][:, 0:nt], start=True, stop=first,
                         reads=['cbf', 'wSP%d' % w3], writes=['ps%d' % pg])
                    if not first:
                        P.op('pe', 'matmul', ps[pg][:, 0:nt], lhsT=ONESNEG, rhs=wACCb[hh][:, 0:nt], start=False, stop=True,
                             reads=['cbf', 'wACCb%d' % hh], writes=['ps%d' % pg])
                    if idx >= 2:
                        stageAV(idx - 2)
                    if not last:
                        if first:
                            P.op('dve', 'tensor_copy', out=wACC[hh][:, 0:nt], in_=wSP[w3][:, 0:nt],
                                 reads=['wSP%d' % w3], writes=['wACC%d' % hh])
                        else:
                            P.op('dve', 'tensor_tensor', out=wACC[hh][:, 0:nt], in0=wACC[hh][:, 0:nt],
                                 in1=wSP[w3][:, 0:nt], op=ALU.add,
                                 reads=['wSP%d' % w3, 'wACC%d' % hh], writes=['wACC%d' % hh])
                        P.op('dve', 'tensor_copy', out=wACCb[hh][:, 0:nt], in_=wACC[hh][:, 0:nt],
                             reads=['wACC%d' % hh], writes=['wACCb%d' % hh])
                    P.op('act', 'activation', out=wG[w][:, 0:nt], in_=ps[pg][:, 0:nt], func=AF.Exp,
                         reads=['ps%d' % pg], writes=['wG%d' % w])
                    P.op('dve', 'tensor_tensor', out=wA[w3][:, 0:nt], in0=wE[idx % 4][:, 0:nt],
                         in1=wG[w][:, 0:nt], op=ALU.mult,
                         reads=['wE%d' % (idx % 4), 'wG%d' % w], writes=['wA%d' % w3])

                def stageAV(idx):
                    c, hh, kb = tiles[idx]
                    KTv, Vv, kvn = kvv[c]
                    w = idx % 3
                    first = (kb == kbs[0])
                    last = (kb == kbs[-1])
                    po = 4 + hh + 2 * (c % 2)
                    P.op('pe', 'matmul', ps[po][:, 0:nt], lhsT=Vv[:, kb, :], rhs=wA[w][:, 0:nt], start=first, stop=last,
                         reads=[kvn, 'wA%d' % w], writes=['ps%d' % po])
                    if last:
                        pb = hh * 64
                        if hh == 0:
                            P.op('act', 'activation', out=hT[pb:pb + 64, 8 + c, 0:nt], in_=ps[po][pb:pb + 64, 0:nt],
                                 func=AF.Copy, reads=['ps%d' % po], writes=['hT'])
                        else:
                            P.op('dve', 'tensor_copy', out=hT[pb:pb + 64, 8 + c, 0:nt], in_=ps[po][pb:pb + 64, 0:nt],
                                 reads=['ps%d' % po], writes=['hT'])
                        if hh == 1 and cont[0] and c + 2 < ncs:
                            load_kv(c + 2)

                def run_pipeline():
                    n = len(tiles)
                    stageA1(0)
                    if n > 1:
                        stageA1(1)
                    stageA2(0)
                    for idx in range(n):
                        if idx + 2 < n:
                            stageA1(idx + 2)
                        if idx + 1 < n:
                            stageA2(idx + 1)
                        stageB(idx)
                    if n >= 2:
                        stageAV(n - 2)
                    stageAV(n - 1)

                cont = [kmax >= 3]
                if ncs > 0 and cont[0]:
                    load_kv(0)
                    load_kv(1)
                    run_pipeline()
                elif ncs > 0:
                    all_tiles = list(tiles)
                    per = 2 * kmax
                    for c in range(ncs):
                        tiles[:] = all_tiles[c * per:(c + 1) * per]
                        load_kv(c)
                        run_pipeline()
                for grp in range(2 if 'n' in P2PARTS else 0):
                    norm_stats(hT[:, grp * 8:(grp + 1) * 8, :], 8, nt, ps[6], 'ps6', 'hT')
                    rstd_from_ps(ps[6], nt, 1.0 / 1024, wRS, 'wG0', 'ps6')
                    for k in range(8):
                        kc = grp * 8 + k
                        si = next_stg()
                        P.op('dve', 'scalar_tensor_tensor', out=stg[si][:, 0:nt], in0=hT[:, kc, 0:nt], scalar=gcol(l, 32 + kc),
                             in1=wRS[:, 0:nt], op0=ALU.mult, op1=ALU.mult, reads=['hT', 'wG0', 'pvec'], writes=['stg%d' % si])
                        P.dma('sp', 'stg%d' % si, out=MIXT[kc, :, t0:t0 + nt], in_=stg[si][:, 0:nt], reads=['stg%d' % si])

        def phase3(l):
            for (b0, ncb) in groups:
                t0 = b0 * 128
                nt = ncb * 128
                P.dma('sp', 'hT', out=hT[:, :, 0:nt], in_=HT[:, :, t0:t0 + nt].rearrange("k p t -> p k t"), writes=['hT'])
                P.dma('sp', 'xT', out=xT[:, :, 0:nt], in_=MIXT[:, :, t0:t0 + nt].rearrange("k p t -> p k t"), writes=['xT'])
                for s in range(4):
                    wv, wname = load_slab(WB_o[l, :, s * 512:(s + 1) * 512], KC, 512)
                    for oc in range(4):
                        pi = next_ps4()
                        for kc in range(KC):
                            P.op('pe', 'matmul', ps[pi][:, 0:nt], lhsT=wv[:, kc, oc * 128:(oc + 1) * 128], rhs=xT[:, kc, 0:nt],
                                 start=(kc == 0), stop=(kc == KC - 1), reads=[wname, 'xT'], writes=['ps%d' % pi])
                        dc = s * 4 + oc
                        P.op('dve', 'tensor_tensor', out=hT[:, dc, 0:nt], in0=hT[:, dc, 0:nt], in1=ps[pi][:, 0:nt], op=ALU.add,
                             reads=['ps%d' % pi, 'hT'], writes=['hT'])
                big_rmsnorm(l, 16, nt)
                for s in range(11):
                    gv, gname = load_slab(WB_g[l, :, s * 512:(s + 1) * 512], KC, 512)
                    uv, uname = load_slab(WB_u[l, :, s * 512:(s + 1) * 512], KC, 512)
                    for oc in range(4):
                        pg = next_ps4()
                        for kc in range(KC):
                            P.op('pe', 'matmul', ps[pg][:, 0:nt], lhsT=gv[:, kc, oc * 128:(oc + 1) * 128], rhs=xT[:, kc, 0:nt],
                                 start=(kc == 0), stop=(kc == KC - 1), reads=[gname, 'xT'], writes=['ps%d' % pg])
                        pu = next_ps4()
                        for kc in range(KC):
                            P.op('pe', 'matmul', ps[pu][:, 0:nt], lhsT=uv[:, kc, oc * 128:(oc + 1) * 128], rhs=xT[:, kc, 0:nt],
                                 start=(kc == 0), stop=(kc == KC - 1), reads=[uname, 'xT'], writes=['ps%d' % pu])
                        fc = s * 4 + oc
                        gi = fc % 2
                        P.op('act', 'activation', out=wSG[gi][:, 0:nt], in_=ps[pg][:, 0:nt], func=AF.Silu,
                             reads=['ps%d' % pg], writes=[nSG_ % gi])
                        P.op('dve', 'tensor_tensor', out=actT[:, fc, 0:nt], in0=wSG[gi][:, 0:nt], in1=ps[pu][:, 0:nt],
                             op=ALU.mult, reads=[nSG_ % gi, 'ps%d' % pu], writes=['actT'])
                for dcg in range(4):
                    for q in range(4):
                        dv, dname = load_slab(WB_d[l, q * 11 * 128:(q + 1) * 11 * 128, dcg * 512:(dcg + 1) * 512], 11, 512)
                        for dcl in range(4):
                            for j in range(11):
                                P.op('pe', 'matmul', ps[4 + dcl][:, 0:nt], lhsT=dv[:, j, dcl * 128:(dcl + 1) * 128],
                                     rhs=actT[:, q * 11 + j, 0:nt], start=(q == 0 and j == 0), stop=(q == 3 and j == 10),
                                     reads=[dname, 'actT'], writes=['ps%d' % (4 + dcl)])
                    for dcl in range(4):
                        dc = dcg * 4 + dcl
                        P.op('dve', 'tensor_tensor', out=hT[:, dc, 0:nt], in0=hT[:, dc, 0:nt], in1=ps[4 + dcl][:, 0:nt],
                             op=ALU.add, reads=['ps%d' % (4 + dcl), 'hT'], writes=['hT'])
                P.dma('sp', 'hst', out=HT[:, :, t0:t0 + nt].rearrange("k p t -> p k t"), in_=hT[:, :, 0:nt], reads=['hT'])

        convert_weights(0)
        phase0()
        P.barrier()
        for l in range(DEPTH):
            wl[0] = l
            if phases is None or 1 in phases:
                phase1(l)
                P.barrier()
            if l + 1 < DEPTH:
                convert_weights(l + 1)
            if phases is None or 2 in phases:
                phase2(l)
                P.barrier()
            if phases is None or 3 in phases:
                phase3(l)
                P.barrier(new_epoch=(l % 2 == 1 and l != DEPTH - 1))
        phaseF()
        P.barrier()
        P.emit()
        nops = {e: len(P.ops[e]) for e in ENGS}
        print("plan ops:", nops, "sems:", len(P.sems))
    return nc


def host_consts(NB):
    h = np.arange(1, 17, dtype=np.float32)
    slopes = np.exp2(-8.0 * h / 16).astype(np.float32)
    s = np.arange(128)[:, None].astype(np.float32)
    t = np.arange(128)[None, :].astype(np.float32)
    bias4 = np.full((128, 4, 16, 128), NEGB, np.float32)
    for hh in range(16):
        sl = slopes[hh]
        d = t - s
        bias4[:, 0, hh, :] = np.where(d >= 0, -sl * d, NEGB)
        d1 = t - s + 128
        bias4[:, 1, hh, :] = np.where(d1 < 128, -sl * d1, NEGB)
        bias4[:, 2, hh, :] = np.where(s >= 112, -sl * d, NEGB)
        bias4[:, 3, hh, :] = np.where((s >= 112) & (d >= 0), -sl * d, NEGB)
    ctab = np.zeros((128, NB, 16), np.float32)
    for b in range(NB):
        ctab[:, b, :] = -slopes[None, :] * 128.0 * b
    sbm = np.zeros((128, 6, 4, 128), np.float32)
    for m in range(4):
        for c in range(4):
            if c > m:
                sbm[:, m, c, :] = 1.0
            elif c == m:
                sbm[:, m, c, :] = (s < t)
    sbm[112:, 4, :, :] = 1.0
    sbm[:, 5, 0, :] = ((s >= 112) & (s < t))
    j = np.arange(128)[:, None]
    si = np.arange(128)[None, :]
    cbf = np.zeros((128, 3, 128), np.float32)
    cbf[:, 0, :] = -(j >= si).astype(np.float32)
    cbf[:, 1, :] = -1.0
    cbf[:, 2, :] = 1.0
    cf32 = np.zeros((128, 3, 128), np.float32)
    cf32[:, 0, :] = 1.0
    cf32[:, 1, :] = ((j // 64) == (si // 64)).astype(np.float32)
    cf32[:, 2, :] = np.eye(128, dtype=np.float32)
    return dict(bias4=bias4.reshape(128, -1), ctab=ctab.reshape(128, -1), sbm=sbm.reshape(128, -1),
                cbf=cbf.reshape(128, -1), cf32=cf32.reshape(128, -1))


def make_inputs(NB, DEPTH, x, meta_tokens, attn_norm_g, w_in, q_norm_g, k_norm_g, attn_sinks,
                swa_out_g, sb_out_g, w_o, ffn_norm_g, w_gate, w_up, w_down):
    consts = host_consts(NB)
    f = lambda a: np.ascontiguousarray(np.asarray(a, dtype=np.float32))
    B = x.shape[0]
    pvec = np.zeros((128, DEPTH * NPV), np.float32)
    for l in range(DEPTH):
        o = l * NPV
        pvec[:, o:o + 16] = f(attn_norm_g[l]).reshape(16, 128).T
        pvec[:, o + 16:o + 32] = f(ffn_norm_g[l]).reshape(16, 128).T
        pvec[:, o + 32:o + 40] = f(swa_out_g[l]).reshape(8, 128).T
        pvec[:, o + 40:o + 48] = f(sb_out_g[l]).reshape(8, 128).T
        pvec[:, o + 48] = np.tile(f(q_norm_g[l]), 2)
        pvec[:, o + 49] = np.tile(f(k_norm_g[l]), 2)
    sinks = f(attn_sinks).reshape(1, DEPTH * 16)
    shared = dict(w_in=f(w_in), w_o=f(w_o), w_gate=f(w_gate), w_up=f(w_up), w_down=f(w_down),
                  pvec=pvec, sinks=sinks, **consts)
    in_maps = []
    xs = {}
    for core in range(8):
        b = core % B
        if b not in xs:
            xi = np.zeros((NB * 128, D), np.float32)
            xi[112:128] = f(meta_tokens)
            xi[128:] = f(x[b])
            xs[b] = xi
        m = dict(shared)
        m["xin"] = xs[b]
        in_maps.append(m)
    return in_maps


_NC_CACHE = {}
PHASES = None
DEBUG = False
import os as _os
P2PARTS = _os.environ.get('P2PARTS', 'abn12345')


def kernel(x, meta_tokens, attn_norm_g, w_in, q_norm_g, k_norm_g, attn_sinks,
           swa_out_g, sb_out_g, w_o, ffn_norm_g, w_gate, w_up, w_down):
    x = np.asarray(x)
    B, S, _ = x.shape
    NB = S // 128 + 1
    DEPTH = np.asarray(w_in).shape[0]
    key = (NB, DEPTH)
    if key not in _NC_CACHE:
        _NC_CACHE[key] = build(NB, DEPTH, debug=DEBUG, phases=PHASES)
    nc = _NC_CACHE[key]
    in_maps = make_inputs(NB, DEPTH, x, meta_tokens, attn_norm_g, w_in, q_norm_g, k_norm_g, attn_sinks,
                          swa_out_g, sb_out_g, w_o, ffn_norm_g, w_gate, w_up, w_down)
    res = run_bass_kernel_spmd(nc, in_maps, core_ids=list(range(8)))
    out = np.stack([np.asarray(res.results[b]["out"], dtype=np.float32) for b in range(B)], axis=0)
    if DEBUG:
        global LAST_RES
        LAST_RES = res.results
    return out
```
